# Optimizing a Trainium2 kernel written in Bass

```python
import math
import jax, jax.numpy as jnp
from jax import lax
import numpy as np

D_MODEL = 1024
BATCH = 8
SEQ = 4096
DEPTH = 1

D_FF = 2816
A_WIDTH = 512
A_GROUPS = 8
A_GROUP_DIM = A_WIDTH // A_GROUPS
CHUNK = 128
B_HEADS = 8
B_QK_DIM = 64
B_V_DIM = 2 * B_QK_DIM
B_WIDTH = B_HEADS * B_V_DIM
ROPE_THETA = 500000.0
ROT_DIM = B_QK_DIM // 4
Q_BLOCK = 128
NORM_EPS = 1e-6
LN_EPS = 1e-5
COL_SIZES = (D_MODEL, D_MODEL, 2 * A_WIDTH,
             B_HEADS * 2 * B_QK_DIM, B_HEADS * 2 * B_QK_DIM, B_WIDTH)
IN_COLS = sum(COL_SIZES)
COL_SPLITS = tuple(int(c) for c in np.cumsum(COL_SIZES)[:-1])

kernel_name = 'hybrid_gmlp_diffattn_macaron'


def rms_norm(x, gain, eps=NORM_EPS):
    xf = x.astype(jnp.float32)
    y = xf * lax.rsqrt(jnp.mean(xf * xf, axis=-1, keepdims=True) + eps)
    return (y * gain.astype(jnp.float32)).astype(x.dtype)


def layer_norm(x, gain, bias, eps=LN_EPS):
    xf = x.astype(jnp.float32)
    mu = jnp.mean(xf, axis=-1, keepdims=True)
    xc = xf - mu
    y = xc * lax.rsqrt(jnp.mean(xc * xc, axis=-1, keepdims=True) + eps)
    return (y * gain.astype(jnp.float32) + bias.astype(jnp.float32)).astype(x.dtype)


def swiglu(h, w_gate, w_up, w_down):
    return (jax.nn.silu(h @ w_gate) * (h @ w_up)) @ w_down


def rope_tables(positions):
    inv = ROPE_THETA ** (-jnp.arange(0, ROT_DIM, 2, dtype=jnp.float32) / ROT_DIM)
    ang = positions.astype(jnp.float32)[..., None] * inv
    return jnp.cos(ang), jnp.sin(ang)


def apply_partial_rope(t, cos, sin):
    half = ROT_DIM // 2
    r1, r2, rest = t[..., :half], t[..., half:ROT_DIM], t[..., ROT_DIM:]
    return jnp.concatenate([r1 * cos - r2 * sin, r2 * cos + r1 * sin, rest], axis=-1)


def gmlp_chunk_mixer(z_uv, ln_g, ln_b, w_s, b_s):
    bsz, seq, _ = z_uv.shape
    z = jax.nn.gelu(z_uv, approximate=False)
    u, v = jnp.split(z, 2, axis=-1)
    v = layer_norm(v, ln_g, ln_b)
    vc = v.reshape(bsz, seq // CHUNK, CHUNK, A_GROUPS, A_GROUP_DIM)
    causal = jnp.tril(jnp.ones((CHUNK, CHUNK), dtype=bool))
    w = jnp.where(causal[None], w_s, jnp.zeros_like(w_s))
    f = jnp.einsum('gts,bcsge->bctge', w, vc) + b_s.T[:, :, None]
    return u * f.reshape(bsz, seq, A_WIDTH)


def diff_attention(q, k, v, positions, q_gain, k_gain, lq1, lk1, lq2, lk2, subln_g, lam_init):
    bsz, seq = q.shape[0], q.shape[1]
    out_dtype = v.dtype
    cos, sin = rope_tables(positions)
    cos, sin = cos[:, :, None, None, :], sin[:, :, None, None, :]
    qf = apply_partial_rope(rms_norm(q.astype(jnp.float32), q_gain), cos, sin)
    kf = apply_partial_rope(rms_norm(k.astype(jnp.float32), k_gain), cos, sin)
    qf = qf * (B_QK_DIM ** -0.5)
    qf = qf.transpose(0, 2, 3, 1, 4)
    kf = kf.transpose(0, 2, 3, 1, 4)
    vf = v.astype(jnp.float32).transpose(0, 2, 1, 3)
    lam = (jnp.exp(jnp.sum(lq1.astype(jnp.float32) * lk1.astype(jnp.float32)))
           - jnp.exp(jnp.sum(lq2.astype(jnp.float32) * lk2.astype(jnp.float32)))
           + lam_init)
    kpos = jnp.arange(seq)

    def block(i):
        start = i * Q_BLOCK
        qb = lax.dynamic_slice_in_dim(qf, start, Q_BLOCK, axis=3)
        s = jnp.einsum('bhmqd,bhmkd->bhmqk', qb, kf)
        qpos = start + jnp.arange(Q_BLOCK)
        mask = kpos[None, :] <= qpos[:, None]
        p = jax.nn.softmax(jnp.where(mask, s, -jnp.inf), axis=-1)
        a = p[:, :, 0] - lam * p[:, :, 1]
        return jnp.einsum('bhqk,bhkd->bhqd', a, vf)

    o = lax.map(block, jnp.arange(seq // Q_BLOCK))
    o = o.transpose(1, 0, 3, 2, 4).reshape(bsz, seq, B_HEADS, B_V_DIM)
    o = rms_norm(o, subln_g) * (1.0 - lam_init)
    return o.reshape(bsz, seq, B_WIDTH).astype(out_dtype)


def setup_inputs(seed: int = 0) -> dict:
    key = jax.random.key(seed)
    ks = jax.random.split(key, 32)
    f32 = jnp.float32
    L = DEPTH

    def nrm(k, shape, scale):
        return jax.random.normal(k, shape, dtype=f32) * scale

    def gain(k, shape):
        return 1.0 + 0.01 * jax.random.normal(k, shape, dtype=f32)

    return {
        'x': jax.random.normal(ks[0], (BATCH, SEQ, D_MODEL), dtype=f32),
        'positions': jnp.broadcast_to(jnp.arange(SEQ, dtype=jnp.int32), (BATCH, SEQ)),
        'ffn1_norm': gain(ks[1], (L, D_MODEL)),
        'ffn1_w_gate': nrm(ks[2], (L, D_MODEL, D_FF), D_MODEL ** -0.5),
        'ffn1_w_up': nrm(ks[3], (L, D_MODEL, D_FF), D_MODEL ** -0.5),
        'ffn1_w_down': nrm(ks[4], (L, D_FF, D_MODEL), D_FF ** -0.5),
        'mix_norm': gain(ks[5], (L, D_MODEL)),
        'w_in': nrm(ks[6], (L, D_MODEL, IN_COLS), D_MODEL ** -0.5),
        'a_ln_gain': gain(ks[7], (L, A_WIDTH)),
        'a_ln_bias': nrm(ks[8], (L, A_WIDTH), 0.01),
        'a_w_s': nrm(ks[9], (L, A_GROUPS, CHUNK, CHUNK), 0.5 * CHUNK ** -0.5),
        'a_b_s': gain(ks[10], (L, A_GROUPS, CHUNK)),
        'a_w_proj': nrm(ks[11], (L, A_WIDTH, D_MODEL), A_WIDTH ** -0.5),
        'b_q_norm': gain(ks[12], (L, B_QK_DIM)),
        'b_k_norm': gain(ks[13], (L, B_QK_DIM)),
        'b_lambda_q1': nrm(ks[14], (L, B_QK_DIM), 0.1),
        'b_lambda_k1': nrm(ks[15], (L, B_QK_DIM), 0.1),
        'b_lambda_q2': nrm(ks[16], (L, B_QK_DIM), 0.1),
        'b_lambda_k2': nrm(ks[17], (L, B_QK_DIM), 0.1),
        'b_subln': gain(ks[18], (L, B_V_DIM)),
        'b_w_proj': nrm(ks[19], (L, B_WIDTH, D_MODEL), B_WIDTH ** -0.5),
        'w_out': nrm(ks[20], (L, D_MODEL, D_MODEL), D_MODEL ** -0.5),
        'ffn2_norm': gain(ks[21], (L, D_MODEL)),
        'ffn2_w_gate': nrm(ks[22], (L, D_MODEL, D_FF), D_MODEL ** -0.5),
        'ffn2_w_up': nrm(ks[23], (L, D_MODEL, D_FF), D_MODEL ** -0.5),
        'ffn2_w_down': nrm(ks[24], (L, D_FF, D_MODEL), D_FF ** -0.5),
    }


def reference(x, positions, ffn1_norm, ffn1_w_gate, ffn1_w_up, ffn1_w_down, mix_norm, w_in,
              a_ln_gain, a_ln_bias, a_w_s, a_b_s, a_w_proj, b_q_norm, b_k_norm,
              b_lambda_q1, b_lambda_k1, b_lambda_q2, b_lambda_k2, b_subln, b_w_proj, w_out,
              ffn2_norm, ffn2_w_gate, ffn2_w_up, ffn2_w_down):
    bsz, seq, _ = x.shape
    for l in range(DEPTH):
        lam_init = 0.8 - 0.6 * math.exp(-0.3 * l)
        x = x + 0.5 * swiglu(rms_norm(x, ffn1_norm[l]), ffn1_w_gate[l], ffn1_w_up[l], ffn1_w_down[l])
        h = rms_norm(x, mix_norm[l])
        z = h @ w_in[l]
        g_a, g_b, z_uv, z_q, z_k, z_v = jnp.split(z, COL_SPLITS, axis=-1)
        y_a = gmlp_chunk_mixer(z_uv, a_ln_gain[l], a_ln_bias[l], a_w_s[l], a_b_s[l])
        q = z_q.reshape(bsz, seq, B_HEADS, 2, B_QK_DIM)
        k = z_k.reshape(bsz, seq, B_HEADS, 2, B_QK_DIM)
        v = z_v.reshape(bsz, seq, B_HEADS, B_V_DIM)
        y_b = diff_attention(q, k, v, positions, b_q_norm[l], b_k_norm[l],
                             b_lambda_q1[l], b_lambda_k1[l], b_lambda_q2[l], b_lambda_k2[l],
                             b_subln[l], lam_init)
        m = jax.nn.sigmoid(g_a) * (y_a @ a_w_proj[l]) + jax.nn.sigmoid(g_b) * (y_b @ b_w_proj[l])
        x = x + m @ w_out[l]
        x = x + 0.5 * swiglu(rms_norm(x, ffn2_norm[l]), ffn2_w_gate[l], ffn2_w_up[l], ffn2_w_down[l])
    return x
```

```python
import math
from contextlib import ExitStack

import numpy as np
import concourse.bass as bass
import concourse.mybir as mybir
from concourse.bass_utils import run_bass_kernel_spmd

F32 = mybir.dt.float32
BF16 = mybir.dt.bfloat16
I32 = mybir.dt.int32
ALU = mybir.AluOpType
AF = mybir.ActivationFunctionType
AX = mybir.AxisListType

D = 1024
DFF = 2816
NFC = DFF // 128
TT = 512
NORM_EPS = 1e-6
LN_EPS = 1e-5
LAM_INIT = 0.8 - 0.6 * math.exp(-0.3 * 0)
ROPE_THETA = 500000.0
NSLAB = 4


class Op:
    __slots__ = ("eng", "fn", "dma_key", "group", "idx", "deps", "dma_n", "signals", "sigval")


class Prog:
    def __init__(self):
        self.ops = []
        self.lastw = {}
        self.rd_eng = {}
        self.rd_dma = {}
        self.dma_count = {}
        self.waitall = set()

    def add(self, eng, fn, reads=(), writes=(), dma_key=None, group=False, waitall=False):
        op = Op()
        op.eng, op.fn, op.dma_key, op.group = eng, fn, dma_key, group
        op.idx = len(self.ops)
        op.deps = {}
        op.dma_n = 0

        def dep(p, kind):
            old = op.deps.get(p)
            if old is None or (old == "war" and kind != "war"):
                op.deps[p] = kind

        for r in reads:
            for w in self.lastw.get(r, ()):
                dep(w, "raw")
        for r in writes:
            for w in self.lastw.get(r, ()):
                if not (group and w.group):
                    dep(w, "waw")
            for p in self.rd_eng.get(r, {}).values():
                dep(p, "war")
            for p in self.rd_dma.get(r, ()):
                dep(p, "war")
        for r in reads:
            if dma_key is not None:
                self.rd_dma.setdefault(r, []).append(op)
            else:
                self.rd_eng.setdefault(r, {})[eng] = op
        for r in writes:
            lw = self.lastw.get(r)
            if group and lw and all(w.group for w in lw):
                lw.append(op)
            else:
                self.lastw[r] = [op]
                self.rd_eng[r] = {}
                self.rd_dma[r] = []
        if dma_key is not None:
            self.dma_count[dma_key] = self.dma_count.get(dma_key, 0) + 1
            op.dma_n = self.dma_count[dma_key]
            if waitall:
                self.waitall.add(dma_key)
        op.signals = dma_key is not None
        self.ops.append(op)
        return op

    @staticmethod
    def _skip(p, op, kind):
        return p.dma_key is None and op.dma_key is None and p.eng == op.eng and p.eng == "pe"

    def finalize(self):
        for op in self.ops:
            for p, kind in op.deps.items():
                if p.dma_key is None and not self._skip(p, op, kind):
                    p.signals = True
        cnt = {}
        for op in self.ops:
            if op.dma_key is None:
                if op.signals:
                    cnt[op.eng] = cnt.get(op.eng, 0) + 1
                op.sigval = cnt.get(op.eng, 0)
            else:
                n = self.dma_count[op.dma_key] if op.dma_key in self.waitall else op.dma_n
                op.sigval = 16 * n

    def sem_keys(self):
        keys = []
        seen = set()
        for op in self.ops:
            k = ("d", op.dma_key) if op.dma_key is not None else ("e", op.eng)
            if k not in seen:
                seen.add(k)
                keys.append(k)
        return keys

    def emit_engine(self, name, eng, sems):
        waited = {}
        for op in self.ops:
            if op.eng != name:
                continue
            needs = {}
            for p, kind in op.deps.items():
                if self._skip(p, op, kind):
                    continue
                key = ("d", p.dma_key) if p.dma_key is not None else ("e", p.eng)
                if needs.get(key, 0) < p.sigval:
                    needs[key] = p.sigval
            for key, v in needs.items():
                if waited.get(key, 0) < v:
                    eng.wait_ge(sems[key], v)
                    waited[key] = v
            if op.fn is None:
                continue
            ins = op.fn(eng)
            if op.dma_key is not None:
                ins.then_inc(sems[("d", op.dma_key)], 16)
            elif op.signals:
                ins.then_inc(sems[("e", name)], 1)


class Builder:
    def __init__(self, nt, debug=()):
        self.nt = nt
        self.S = nt * TT
        self.debug = set(debug)
        self.P = Prog()
        self.nc = bass.Bass("TRN2", target_bir_lowering=False)
        self.slab_ctr = 0
        self.dbg_outs = {}
        self._alloc()

    def din(self, name, shape, dt=F32):
        return self.nc.dram_tensor(name, list(shape), dt, kind="ExternalInput").ap()

    def dscratch(self, name, shape, dt=BF16):
        return self.nc.dram_tensor(name, list(shape), dt).ap()

    def sb(self, name, shape, dt):
        return self.nc.alloc_sbuf_tensor(name, list(shape), dt)

    def _alloc(self):
        nc, S, nt = self.nc, self.S, self.nt
        self.x_d = self.din("x", [S, D])
        self.out_d = nc.dram_tensor("out", [S, D], F32, kind="ExternalOutput").ap()
        self.pos_d = self.din("pos_t", [128, nt * 4], I32)
        wshapes = {
            "wg1": (D, DFF), "wu1": (D, DFF), "wd1": (DFF, D),
            "win": (D, 6144), "pa": (512, D), "pb": (D, D), "wo": (D, D),
            "wg2": (D, DFF), "wu2": (D, DFF), "wd2": (DFF, D),
        }
        self.w_f32 = {k: self.din(k, v) for k, v in wshapes.items()}
        self.w_bf = {k: self.dscratch(k + "_bf", v) for k, v in wshapes.items()}
        self.wshapes = wshapes
        self.cin = {
            "g1": self.din("g1", [128, 8]), "gm": self.din("gm", [128, 8]), "g2": self.din("g2", [128, 8]),
            "gq_t": self.din("gq_t", [128, 16]), "gk_t": self.din("gk_t", [128, 16]), "gqk_col": self.din("gqk_col", [128, 2]),
            "lng_t": self.din("lng_t", [128, 512]), "lnb_t": self.din("lnb_t", [128, 512]),
            "bsT": self.din("bsT", [128, 512]), "ws": self.din("ws", [8, 128, 128]),
            "lamv": self.din("lamv", [128, 256]), "subln": self.din("subln", [128, 1]),
        }
        self.kc_d = self.dscratch("kcache", [8, 128, S])
        self.vc_d = self.dscratch("vcache", [S, D])
        self.xT = self.sb("xT", [128, 8, TT], F32)
        self.hT = self.sb("hT", [128, 8, TT], BF16)
        self.arena = self.sb("arena", [128, NFC, TT], BF16)
        self.yaT = self.sb("yaT", [128, 4, TT], BF16)
        self.ybT = self.sb("ybT", [128, 8, TT], BF16)
        self.QT = self.sb("QT", [128, 8, TT], BF16)
        self.KTc = self.sb("KTc", [128, 8, TT], BF16)
        self.Vc = self.sb("Vc", [128, 4, D], BF16)
        self.vn = self.sb("vn", [128, 4, 512], BF16)
        self.SL = [self.sb(f"slab{i}", [128, 4096], BF16) for i in range(NSLAB)]
        self.kp = [self.sb(f"kp{i}", [128, max(S - TT, 128)], BF16) for i in range(2)]
        self.vp = [self.sb(f"vp{i}", [128, max(nt * 4 - 4, 1), 128], BF16) for i in range(2)]
        self.PTA = self.sb("pta", [128, 2, 2, TT], BF16)
        self.stg = [self.sb(f"stg{i}", [128, D], F32) for i in range(2)]
        self.SF = [self.sb(f"sf{i}", [128, TT], F32) for i in range(8)]
        self.cosT = self.SF[6][:, 0:nt * 64].rearrange("p (s e) -> p s e", e=16)
        self.sinT = self.SF[7][:, 0:nt * 64].rearrange("p (s e) -> p s e", e=16)
        self.SBb = [self.sb(f"sbb{i}", [128, TT], BF16) for i in range(2)]
        self.sm = self.sb("sm", [128, 256], F32)
        self.rt = [self.sb(f"rt{i}", [128, 128], F32) for i in range(8)]
        self.ident = self.sb("ident", [128, 128], F32)
        self.ones_f = self.sb("ones_f", [128, 128], F32)
        self.tri = self.sb("tri", [128, 128], F32)
        self.ones_b = self.sb("ones_b", [128, 128], BF16)
        self.onesbig = self.sb("onesbig", [128, TT], BF16)
        self.maskA = self.sb("maskA", [128, 4, TT], BF16)
        self.cg = {k: self.sb("c_" + k, [128, 8], F32) for k in ("g1", "gm", "g2")}
        self.gq_t = self.sb("gq_ts", [128, 16], F32)
        self.gk_t = self.sb("gk_ts", [128, 16], F32)
        self.gcol = self.sb("gcol", [128, 2], F32)
        self.ropeT = {k: self.sb("rope_" + k, [128, nt * 4, 32], F32) for k in "qk"}
        self.lng_t = self.sb("lng_ts", [128, 512], F32)
        self.lnb_t = self.sb("lnb_ts", [128, 512], F32)
        self.bsT = self.sb("bsTs", [128, 512], F32)
        self.wsT = self.sb("wsT", [128, 8, 128], BF16)
        self.lamv = self.sb("lamvs", [128, 256], F32)
        self.csm = self.sb("csm", [128, 16], F32)
        self.pos_i = self.sb("pos_i", [128, nt * 4], I32)
        self.rpi = self.sb("rpi", [128, nt * 4, 8], I32)
        self.PSA = nc.alloc_psum_tensor("psall", [128, 8 * TT], F32)
        self.PS = [self.PSA[:, i * TT:(i + 1) * TT] for i in range(8)]

    C_EPS, C_EPSLN, C_NEGLAM, C_GSUB, C_T0, C_T1, C_T2, C_T3, C_HALFPI = range(9)

    def cc(self, i):
        return self.csm[:, i:i + 1]

    def mm(self, out, lhsT, rhs, start, stop, reads, writes):
        self.P.add("pe", lambda e: e.matmul(out, lhsT, rhs, start=start, stop=stop), reads, writes)

    def tr(self, out, in_, reads, writes):
        ident = self.ident[:, :]
        self.P.add("pe", lambda e: e.transpose(out, in_, ident), list(reads) + ["ident"], writes)

    def act(self, out, in_, func, reads, writes, bias=None, scale=None):
        kw = {}
        if bias is not None:
            kw["bias"] = bias
        if scale is not None:
            kw["scale"] = scale
        self.P.add("act", lambda e: e.activation(out=out, in_=in_, func=func, **kw), reads, writes)

    def tt(self, eng, out, in0, in1, op, reads, writes):
        self.P.add(eng, lambda e: e.tensor_tensor(out=out, in0=in0, in1=in1, op=op), reads, writes)

    def ts(self, eng, out, in0, s1, s2, op0, op1, reads, writes):
        if op1 is None:
            self.P.add(eng, lambda e: e.tensor_scalar(out=out, in0=in0, scalar1=s1, scalar2=None, op0=op0), reads, writes)
        else:
            self.P.add(eng, lambda e: e.tensor_scalar(out=out, in0=in0, scalar1=s1, scalar2=s2, op0=op0, op1=op1), reads, writes)

    def stt(self, out, in0, scalar, in1, op0, op1, reads, writes):
        self.P.add("dve", lambda e: e.scalar_tensor_tensor(out=out, in0=in0, scalar=scalar, in1=in1, op0=op0, op1=op1), reads, writes)

    def copy(self, eng, out, in_, reads, writes):
        if eng == "act":
            self.P.add("act", lambda e: e.copy(out=out, in_=in_), reads, writes)
        else:
            self.P.add(eng, lambda e: e.tensor_copy(out=out, in_=in_), reads, writes)

    def rsqrt_act(self, out, in_, scale, eps_ap, eps_res, tmp, reads, rtmp, rout, power=-0.5):
        if eps_ap is None:
            self.act(tmp, in_, AF.Ln, reads, [rtmp], scale=scale)
        else:
            self.act(tmp, in_, AF.Ln, list(reads) + [eps_res], [rtmp], bias=eps_ap, scale=scale)
        self.act(out, tmp, AF.Exp, [rtmp], [rout], scale=power)

    def recip(self, out, in_, reads, writes):
        self.P.add("dve", lambda e: e.reciprocal(out=out, in_=in_), reads, writes)

    def memset(self, eng, ap, val, writes):
        self.P.add(eng, lambda e: e.memset(ap, val), (), writes)

    def dma(self, q, out, in_, reads, writes, key, group=False, waitall=False):
        self.P.add(q, lambda e: e.dma_start(out=out, in_=in_), reads, writes, dma_key=key, group=group, waitall=waitall)

    def dbg(self, name, ap, shape, reads, dt=F32):
        if name not in self.debug:
            return
        d = self.nc.dram_tensor("dbg_" + name, list(shape), dt, kind="ExternalOutput").ap()
        self.dbg_outs[name] = d
        self.dma("sp", d, ap, reads, ["OUT"], key=("dbg", name), group=True)

    def setup(self):
        nt = self.nt
        cl = [("g1", self.cg["g1"]), ("gm", self.cg["gm"]), ("g2", self.cg["g2"]), ("gq_t", self.gq_t), ("gk_t", self.gk_t),
              ("lng_t", self.lng_t), ("lnb_t", self.lnb_t), ("bsT", self.bsT), ("lamv", self.lamv), ("gqk_col", self.gcol)]
        for k, t in cl:
            self.dma("sp", t[:, :], self.cin[k], (), ["c_" + k], key="consts", group=True, waitall=True)
        self.dma("sp", self.csm[:, self.C_GSUB:self.C_GSUB + 1], self.cin["subln"], (), ["c_subln"], key="consts", group=True, waitall=True)
        self.dma("sp", self.pos_i[:, :], self.pos_d, (), ["c_pos"], key="consts", group=True, waitall=True)
        for half in range(2):
            dst = self.SF[half][:, :].rearrange("p (g s) -> p g s", g=4)
            src = self.cin["ws"][half * 4:(half + 1) * 4, :, :].rearrange("g t s -> t g s")
            self.dma("sp", dst, src, (), [("S", half)], key=("wsld", half))
        self.memset("pool", self.ones_f[:, :], 1.0, ["ones_f"])
        self.memset("pool", self.ones_b[:, :], 1.0, ["ones_b"])
        self.memset("pool", self.onesbig[:, :], 1.0, ["onesbig"])
        ones_f, ident, tri = self.ones_f[:, :], self.ident[:, :], self.tri[:, :]
        self.P.add("pool", lambda e: e.affine_select(out=ident, in_=ones_f, pattern=[[1, 128]], compare_op=ALU.is_equal,
                                                     fill=0.0, base=0, channel_multiplier=-1), ["ones_f"], ["ident"])
        self.P.add("pool", lambda e: e.affine_select(out=tri, in_=ones_f, pattern=[[1, 128]], compare_op=ALU.is_ge,
                                                     fill=0.0, base=0, channel_multiplier=-1), ["ones_f"], ["tri"])
        for a in range(4):
            self._mask(a)
        def conv(name, gi, c0, c1):
            self.dma("pool", self.w_bf[name][:, c0:c1], self.w_f32[name][:, c0:c1], (), [("wbf", name, gi)], key=("cv", name, gi))
        ffn_groups = [(0, 4), (4, 4), (8, 4), (12, 4), (16, 4), (20, 2)]
        def conv_ffn(which):
            for gi, (j0, nj) in enumerate(ffn_groups):
                conv(f"wg{which}", gi, j0 * 128, (j0 + nj) * 128)
                conv(f"wu{which}", gi, j0 * 128, (j0 + nj) * 128)
            for gi in range(2):
                conv(f"wd{which}", gi, gi * 512, (gi + 1) * 512)
        conv_ffn(1)
        for gi in range(12):
            conv("win", gi, gi * 512, (gi + 1) * 512)
        conv("pa", 0, 0, 1024)
        for gi in range(2):
            conv("pb", gi, gi * 512, (gi + 1) * 512)
        for gi in range(2):
            conv("wo", gi, gi * 512, (gi + 1) * 512)
        conv_ffn(2)
        self.memset("dve", self.cc(self.C_EPS), NORM_EPS, ["c_eps"])
        self.memset("dve", self.cc(self.C_EPSLN), LN_EPS, ["c_epsln"])
        self.memset("dve", self.cc(self.C_HALFPI), math.pi / 2, ["c_halfpi"])
        self.ts("dve", self.gq_t[:, :], self.gq_t[:, :], 0.125, None, ALU.mult, None, ["c_gq_t"], ["c_gq_t"])
        self.ts("dve", self.gcol[:, 0:1], self.gcol[:, 0:1], 0.125, None, ALU.mult, None, ["c_gqk_col"], ["c_gqk_col"])
        for p0 in (0, 64):
            self.memset("dve", self.gcol[p0:p0 + 16, :], 1.0, ["c_gqk_col"])
        self.ts("dve", self.cc(self.C_GSUB), self.cc(self.C_GSUB), 1.0 - LAM_INIT, None, ALU.mult, None, ["c_subln"], ["c_subln"])
        lv = self.lamv
        self.tt("dve", self.sm[:, 0:64], lv[:, 0:64], lv[:, 64:128], ALU.mult, ["c_lamv"], [("sm", 0)])
        t0 = self.cc(self.C_T0)
        self.P.add("dve", lambda e: e.tensor_reduce(out=t0, in_=self.sm[:, 0:64], axis=AX.X, op=ALU.add), [("sm", 0)], ["c_t0"])
        self.tt("dve", self.sm[:, 0:64], lv[:, 128:192], lv[:, 192:256], ALU.mult, ["c_lamv", "c_t0"], [("sm", 0)])
        t1 = self.cc(self.C_T1)
        self.P.add("dve", lambda e: e.tensor_reduce(out=t1, in_=self.sm[:, 0:64], axis=AX.X, op=ALU.add), [("sm", 0)], ["c_t1"])
        self.act(self.cc(self.C_T2), t0, AF.Exp, ["c_t0"], ["c_t2"])
        self.act(self.cc(self.C_T3), t1, AF.Exp, ["c_t1"], ["c_t3"])
        self.tt("dve", self.cc(self.C_NEGLAM), self.cc(self.C_T3), self.cc(self.C_T2), ALU.subtract, ["c_t2", "c_t3"], ["c_neglam"])
        self.ts("dve", self.cc(self.C_NEGLAM), self.cc(self.C_NEGLAM), -LAM_INIT, None, ALU.add, None, ["c_neglam"], ["c_neglam"])
        for g in range(8):
            half, gi = g // 4, g % 4
            b = g // 4
            self.tr(self.PS[b][:, gi * 128:(gi + 1) * 128], self.SF[half][:, gi * 128:(gi + 1) * 128], [("S", half)], [("B", b)])
            self.tt("dve", self.wsT[:, g, :], self.PS[b][:, gi * 128:(gi + 1) * 128], self.tri[:, :], ALU.mult, [("B", b), "tri"], ["wsT"])
        self._rope_tables()
        self._rope_gain_tables()

    def _mask(self, a):
        out = self.maskA[:, a, :]
        ob = self.onesbig[:, :]
        self.P.add("pool", lambda e: e.affine_select(out=out, in_=ob, pattern=[[1, TT]], compare_op=ALU.is_ge, fill=0.0,
                                                     base=-128 * a, channel_multiplier=-1), ["onesbig"], ["maskA"])

    def _rope_tables(self):
        nst = self.nt * 4
        inv = (np.float32(ROPE_THETA) ** (-np.arange(0, 16, 2, dtype=np.float32) / np.float32(16))).astype(np.float32)
        ang, kf, r = [self.SF[2 + i][:, 0:nst * 8].rearrange("p (s e) -> p s e", e=8) for i in range(3)]
        posf = self.sm[:, 0:nst]
        self.copy("dve", posf, self.pos_i[:, :], ["c_pos", ("sm", 0)], [("sm", 0)])
        for i in range(8):
            self.ts("dve", ang[:, :, i], posf, float(inv[i]), None, ALU.mult, None, [("sm", 0)], [("S", 2)])
        C1 = 6.28125
        C2 = 2 * math.pi - C1
        self.ts("dve", kf[:, :, :], ang[:, :, :], 1.0 / (2 * math.pi), None, ALU.mult, None, [("S", 2)], [("S", 3)])
        self.copy("dve", self.rpi[:, :, :], kf[:, :, :], [("S", 3)], ["rpi"])
        self.copy("dve", kf[:, :, :], self.rpi[:, :, :], ["rpi"], [("S", 3)])
        self.stt(r[:, :, :], kf[:, :, :], -C1, ang[:, :, :], ALU.mult, ALU.add, [("S", 2), ("S", 3)], [("S", 4)])
        self.stt(r[:, :, :], kf[:, :, :], -C2, r[:, :, :], ALU.mult, ALU.add, [("S", 3), ("S", 4)], [("S", 4)])
        self.ts("dve", kf[:, :, :], r[:, :, :], math.pi, -2 * math.pi, ALU.is_gt, ALU.mult, [("S", 4)], [("S", 3)])
        self.tt("dve", r[:, :, :], r[:, :, :], kf[:, :, :], ALU.add, [("S", 3), ("S", 4)], [("S", 4)])
        self.ts("dve", kf[:, :, :], r[:, :, :], -math.pi, 2 * math.pi, ALU.is_lt, ALU.mult, [("S", 4)], [("S", 3)])
        self.tt("dve", r[:, :, :], r[:, :, :], kf[:, :, :], ALU.add, [("S", 3), ("S", 4)], [("S", 4)])
        self.ts("dve", r[:, :, :], r[:, :, :], 3.1415925, -3.1415925, ALU.min, ALU.max, [("S", 4)], [("S", 4)])
        self.act(self.sinT[:, :, 8:16], r[:, :, :], AF.Sin, [("S", 4)], [("S", 7)])
        self.act(self.sinT[:, :, 0:8], r[:, :, :], AF.Sin, [("S", 4)], [("S", 7)], scale=-1.0)
        self.act(ang[:, :, :], r[:, :, :], AF.Abs, [("S", 4), ("S", 2)], [("S", 2)])
        self.act(self.cosT[:, :, 0:8], ang[:, :, :], AF.Sin, [("S", 2), "c_halfpi"], [("S", 6)], bias=self.cc(self.C_HALFPI), scale=-1.0)
        self.act(self.cosT[:, :, 8:16], ang[:, :, :], AF.Sin, [("S", 2), "c_halfpi"], [("S", 6)], bias=self.cc(self.C_HALFPI), scale=-1.0)

    def _rope_gain_tables(self):
        nst = self.nt * 4
        for k, gt, gres in (("q", self.gq_t, "c_gq_t"), ("k", self.gk_t, "c_gk_t")):
            T = self.ropeT[k]
            g1 = gt[:, 0:8].unsqueeze(1).to_broadcast([128, nst, 8])
            g2 = gt[:, 8:16].unsqueeze(1).to_broadcast([128, nst, 8])
            rr = [("S", 6), ("S", 7), gres]
            self.tt("dve", T[:, :, 0:8], self.cosT[:, :, 0:8], g1, ALU.mult, rr, ["ropeT" + k])
            self.tt("dve", T[:, :, 8:16], self.cosT[:, :, 8:16], g2, ALU.mult, rr, ["ropeT" + k])
            self.tt("dve", T[:, :, 16:24], self.sinT[:, :, 0:8], g2, ALU.mult, rr, ["ropeT" + k])
            self.tt("dve", T[:, :, 24:32], self.sinT[:, :, 8:16], g1, ALU.mult, rr, ["ropeT" + k])

    def slab(self, wname, gi, src, shape3):
        i = self.slab_ctr % NSLAB
        self.slab_ctr += 1
        a, b = shape3
        view = self.SL[i][:, 0:a * b].rearrange("p (a b) -> p a b", a=a)
        self.dma("sp", view, src, [("wbf", wname, gi)], [("slab", i)], key=("slab", i))
        return view, ("slab", i)

    def wslab(self, wname, c0, ncols):
        w = self.w_bf[wname]
        kc = self.wshapes[wname][0] // 128
        src = w[:, c0:c0 + ncols].rearrange("(c p) f -> p c f", p=128)
        return self.slab(wname, 0 if wname == "pa" else c0 // 512, src, (kc, ncols))

    def xstage(self, s):
        t, nm = (self.QT, "QT") if s < 2 else (self.KTc, "KTc")
        h0 = (s % 2) * 4
        view = t[:, h0:h0 + 4, :].bitcast(F32).rearrange("p a b -> p (a b)")
        return view, [(nm, h) for h in range(h0, h0 + 4)]

    def load_x_dma(self, j):
        for s in range(4):
            view, res = self.xstage(s)
            r0 = j * TT + s * 128
            self.dma("sp", view, self.x_d[r0:r0 + 128, :], (), res, key=("xld", s))

    def load_x_tr(self, j):
        for s in range(4):
            view, res = self.xstage(s)
            for half in range(2):
                b = half
                for ci in range(4):
                    c = half * 4 + ci
                    self.tr(self.PS[b][:, ci * 128:(ci + 1) * 128], view[:, c * 128:(c + 1) * 128], res, [("B", b)])
                self.copy("act", self.xT[:, half * 4:(half + 1) * 4, s * 128:(s + 1) * 128],
                          self.PS[b][:, :].rearrange("p (c t) -> p c t", c=4), [("B", b)], [("xT", c) for c in range(half * 4, half * 4 + 4)])

    def store_x(self, j):
        for s in range(4):
            st = self.stg[s % 2]
            for half in range(2):
                b = 6 + half
                for ci in range(4):
                    c = half * 4 + ci
                    self.tr(self.PS[b][:, ci * 128:(ci + 1) * 128], self.xT[:, c, s * 128:(s + 1) * 128], [("xT", c)], [("B", b)])
                self.copy("act", st[:, half * 512:(half + 1) * 512], self.PS[b][:, :], [("B", b)], [("stg", s % 2)])
            r0 = j * TT + s * 128
            self.dma("sp", self.out_d[r0:r0 + 128, :], st[:, :], [("stg", s % 2)], ["OUT"], key=("stg", s % 2), group=True)

    def sumsq_sq(self, c):
        sq = self.SBb[c % 2]
        self.act(sq[:, :], self.xT[:, c, :], AF.Square, [("xT", c)], [("SB", c % 2)])

    def sumsq_mm(self, c):
        sq = self.SBb[c % 2]
        self.mm(self.PS[2][:, :], self.ones_b[:, :], sq[:, :], c == 0, c == 7, [("SB", c % 2), "ones_b"], [("B", 2)])

    def norm(self, gname, presummed=False):
        g = self.cg[gname]
        if not presummed:
            for c in range(8):
                self.sumsq_sq(c)
                self.sumsq_mm(c)
        sd, rs = self.SF[6], self.SF[7]
        self.rsqrt_act(rs[:, :], self.PS[2][:, :], 1.0 / D, self.cc(self.C_EPS), "c_eps", sd[:, :], [("B", 2)], ("S", 6), ("S", 7))
        for c in range(8):
            self.stt(self.hT[:, c, :], self.xT[:, c, :], g[:, c:c + 1], rs[:, :], ALU.mult, ALU.mult,
                     [("xT", c), ("S", 7), "c_" + gname], [("hT", c)])

    def ffn(self, which):
        wg, wu, wd = f"wg{which}", f"wu{which}", f"wd{which}"
        self.norm(f"g{which}", presummed=(which == 2))
        hh = self.arena
        groups = [(0, 4), (4, 4), (8, 4), (12, 4), (16, 4), (20, 2)]
        for (j0, nj) in groups:
            sg, rg = self.wslab(wg, j0 * 128, nj * 128)
            su, ru = self.wslab(wu, j0 * 128, nj * 128)
            for jj in range(nj):
                j = j0 + jj
                bg, bu = (3, 4) if j % 2 == 0 else (5, 6)
                for c in range(8):
                    self.mm(self.PS[bg][:, :], sg[:, c, jj * 128:(jj + 1) * 128], self.hT[:, c, :], c == 0, c == 7,
                            [rg, ("hT", c)], [("B", bg)])
                for c in range(8):
                    self.mm(self.PS[bu][:, :], su[:, c, jj * 128:(jj + 1) * 128], self.hT[:, c, :], c == 0, c == 7,
                            [ru, ("hT", c)], [("B", bu)])
                sl = j % 2
                self.act(self.SF[sl][:, :], self.PS[bg][:, :], AF.Silu, [("B", bg)], [("S", sl)])
                self.tt("dve", hh[:, j, :], self.SF[sl][:, :], self.PS[bu][:, :], ALU.mult, [("S", sl), ("B", bu)], [("A", j)])
        for c in range(8):
            src = self.w_bf[wd][:, c * 128:(c + 1) * 128].rearrange("(j p) m -> p j m", p=128)
            sdv, rd = self.slab(wd, c // 4, src, (NFC, 128))
            b = c % 2
            for j in range(NFC):
                self.mm(self.PS[b][:, :], sdv[:, j, :], hh[:, j, :], j == 0, j == NFC - 1, [rd, ("A", j)], [("B", b)])
            if which == 1 and c > 0:
                self.sumsq_mm(c - 1)
            self.stt(self.xT[:, c, :], self.PS[b][:, :], 0.5, self.xT[:, c, :], ALU.mult, ALU.add, [("B", b), ("xT", c)], [("xT", c)])
            if which == 1:
                self.sumsq_sq(c)
        if which == 1:
            self.sumsq_mm(7)

    def sgA(self, c):
        return self.arena[:, c, :], ("A", c)

    def sgB(self, c):
        return self.arena[:, 8 + c, :], ("A", 8 + c)

    def uT(self, c):
        return self.arena[:, 16 + c, :], ("A", 16 + c)

    def mixer_in(self, j):
        self.norm("gm", presummed=True)
        bank_rr = 0
        for grp in range(5):
            sv, rsl = self.wslab("win", grp * 512, 512)
            for ci in range(4):
                cc = grp * 4 + ci
                b = 3 + (bank_rr % 4)
                bank_rr += 1
                for c in range(8):
                    self.mm(self.PS[b][:, :], sv[:, c, ci * 128:(ci + 1) * 128], self.hT[:, c, :], c == 0, c == 7,
                            [rsl, ("hT", c)], [("B", b)])
                if cc < 8:
                    o, r = self.sgA(cc)
                    self.act(o, self.PS[b][:, :], AF.Sigmoid, [("B", b)], [r])
                elif cc < 16:
                    o, r = self.sgB(cc - 8)
                    self.act(o, self.PS[b][:, :], AF.Sigmoid, [("B", b)], [r])
                else:
                    o, r = self.uT(cc - 16)
                    self.act(o, self.PS[b][:, :], AF.Gelu, [("B", b)], [r])
        kinds = ["vg", "q0", "q1", "k0", "k1", "v0", "v1"]
        units = [(ti, kind, s) for ti, kind in enumerate(kinds) for s in range(4)]
        stageB, stageC = {}, {}
        nqk = 0
        vgB = []
        sv = rsl = None
        for t in range(len(units) + 3):
            if t - 3 in stageC:
                stageC.pop(t - 3)()
            A1 = A2 = None
            if t < len(units):
                ti, kind, s = units[t]
                if s == 0:
                    sv, rsl = self.wslab("win", 2560 + ti * 512, 512)
                b = 3 + (t % 5)
                st_ = t % 4
                for c in range(8):
                    self.mm(self.PS[b][:, :], self.hT[:, c, s * 128:(s + 1) * 128], sv[:, c, :], c == 0, c == 7,
                            [rsl, ("hT", c)], [("B", b)])
                if kind == "vg":
                    A1, Bv = self.vg_stages(b, s, s)
                    vgB.append(Bv)
                    if s == 3:
                        A2 = self.vg_rstd_all
                elif kind[0] in "qk":
                    A1, A2, B, C = self.qk_stages(b, s, int(kind[1]), kind[0], j, st_, nqk % 3)
                    nqk += 1
                    stageB[t], stageC[t] = B, C
                else:
                    half = int(kind[1])
                    self.copy("act", self.Vc[:, s, half * 512:(half + 1) * 512], self.PS[b][:, :], [("B", b)], [("Vc", s)])
            if t in (4, 5):
                vgB.pop(0)()
                vgB.pop(0)()
            if t - 2 in stageB:
                stageB.pop(t - 2)()
            if A1 is not None:
                A1()
            if A2 is not None:
                A2()
        if j < self.nt - 1:
            self.dma("sp", self.kc_d[:, :, j * TT:(j + 1) * TT].rearrange("h p t -> p h t"), self.KTc[:, :, :],
                     [("KTc", h) for h in range(8)], [("kc", j)], key="kcw")
            self.dma("sp", self.vc_d[j * TT:(j + 1) * TT, :].rearrange("(s p) d -> p s d", p=128), self.Vc[:, :, :],
                     [("Vc", s) for s in range(4)], [("vc", j)], key="vcw")

    def vg_stages(self, b, s, i):
        vgf = self.SF[2 * i + 1]
        r0 = ("S", 2 * i + 1)
        smr = ("sm", i)
        o = i * 64
        st6 = self.sm[:, o:o + 6]
        mv = self.sm[:, o + 8:o + 10]
        rs = self.sm[:, o + 11:o + 12]

        def A1():
            self.act(vgf[:, :], self.PS[b][:, :], AF.Gelu, [("B", b)], [r0])
            self.P.add("dve", lambda e: e.bn_stats(out=st6, in_=vgf[:, :]), [r0], [smr])
            self.P.add("dve", lambda e: e.bn_aggr(out=mv, in_=st6), [smr], [smr])

        def B():
            self.ts("dve", vgf[:, :], vgf[:, :], self.sm[:, o + 8:o + 9], rs, ALU.subtract, ALU.mult, [r0, smr], [r0])
            self.tt("dve", vgf[:, :], vgf[:, :], self.lng_t[:, :], ALU.mult, [r0, "c_lng_t"], [r0])
            self.tt("dve", self.vn[:, s, :], vgf[:, :], self.lnb_t[:, :], ALU.add, [r0, "c_lnb_t"], [("vn", s)])
        return A1, B

    def vg_rstd_all(self):
        smv = self.sm[:, :].rearrange("p (i c) -> p i c", c=64)
        var, sd, rs = smv[:, :, 9], smv[:, :, 10], smv[:, :, 11]
        allsm = [("sm", i) for i in range(4)]
        self.act(sd, var, AF.Ln, allsm + ["c_epsln"], allsm, bias=self.cc(self.C_EPSLN), scale=1.0)
        self.act(rs, sd, AF.Exp, allsm, allsm, scale=-0.5)

    def qk_stages(self, b, s, half, kind, j, st_, tb):
        ps = self.PS[b]
        i0 = st_ * 2
        sq, zn = self.SF[i0], self.SF[i0 + 1]
        rq, rn = ("S", i0), ("S", i0 + 1)
        smr = ("sm", st_)
        o = st_ * 64
        ki = 0 if kind == "q" else 1
        ss = self.sm[:, o + 16:o + 24]
        sd = self.sm[:, o + 24:o + 32]
        rs = self.sm[:, o + 32:o + 40]

        def A1():
            self.act(sq[:, :], ps[:, :], AF.Square, [("B", b)], [rq])
            self.P.add("dve", lambda e: e.tensor_reduce(out=ss, in_=sq[:, :].rearrange("p (g e) -> p g e", e=64), axis=AX.X, op=ALU.add),
                       [rq], [smr])

        def A2():
            self.rsqrt_act(rs, ss, 1.0 / 64, self.cc(self.C_EPS), "c_eps", sd, [smr], smr, smr)
            self.tt("dve", zn[:, :].rearrange("p (g e) -> p g e", e=64), ps[:, :].rearrange("p (g e) -> p g e", e=64),
                    rs.unsqueeze(2).to_broadcast([128, 8, 64]), ALU.mult, [("B", b), smr], [rn])

        def B():
            st = j * 4 + s
            z3 = zn[:, :].rearrange("p (g e) -> p g e", e=64)
            r1, r2, z16 = z3[:, :, 0:8], z3[:, :, 8:16], z3[:, :, 0:16]
            T = self.ropeT[kind]
            tres = "ropeT" + kind
            cg = T[:, st:st + 1, 0:16].to_broadcast([128, 8, 16])
            nsg = T[:, st:st + 1, 16:24].to_broadcast([128, 8, 8])
            psg = T[:, st:st + 1, 24:32].to_broadcast([128, 8, 8])
            t0 = self.rt[st_ * 2][:, :].rearrange("p (g e) -> p g e", e=16)
            u = self.rt[st_ * 2 + 1][:, :].rearrange("p (g e) -> p g e", e=16)
            rt0, rt1 = ("rt", st_, 0), ("rt", st_, 1)
            self.tt("dve", t0, z16, cg, ALU.mult, [rn, tres], [rt0])
            self.tt("dve", u[:, :, 0:8], r2, nsg, ALU.mult, [rn, tres], [rt1])
            self.tt("dve", u[:, :, 8:16], r1, psg, ALU.mult, [rn, tres], [rt1])
            self.tt("dve", z16, t0, u, ALU.add, [rt0, rt1], [rn])

        def C():
            for hh in range(4):
                self.tr(self.PS[tb][:, hh * 128:(hh + 1) * 128], zn[:, hh * 128:(hh + 1) * 128], [rn], [("B", tb)])
            dst = self.QT if kind == "q" else self.KTc
            dn = "QT" if kind == "q" else "KTc"
            self.act(dst[:, half * 4:(half + 1) * 4, s * 128:(s + 1) * 128], self.PS[tb][:, :].rearrange("p (h t) -> p h t", h=4),
                     AF.Identity, [("B", tb), "c_gqk_col"], [(dn, h) for h in range(half * 4, half * 4 + 4)], scale=self.gcol[:, ki:ki + 1])
        return A1, A2, B, C

    def gmlp(self):
        for s in range(4):
            b = 7 if s % 2 == 0 else 2
            for g in range(8):
                gp, half = g // 2, g % 2
                self.mm(self.PS[b][half * 64:(half + 1) * 64, gp * 128:(gp + 1) * 128], self.vn[:, s, g * 64:(g + 1) * 64],
                        self.wsT[:, g, :], True, True, [("vn", s), "wsT"], [("B", b)])
            tmp = self.SF[5]
            self.tt("dve", tmp[:, :], self.PS[b][:, :], self.bsT[:, :], ALU.add, [("B", b), "c_bsT"], [("S", 5)])
            ur = [("A", 16 + c) for c in range(4)]
            self.tt("dve", self.yaT[:, :, s * 128:(s + 1) * 128], tmp[:, :].rearrange("p (g t) -> p g t", g=4),
                    self.arena[:, 16:20, s * 128:(s + 1) * 128], ALU.mult, [("S", 5)] + ur, [("yaT", c) for c in range(4)])

    def attention(self, j):
        nprev = 4 * j
        nkt = nprev + 4
        post = None
        sched = {1: 1, 2: 2, 3: 3} if nkt < 8 else {1: 1, 4: 2, 5: 3}
        for h in range(8):
            kb = h % 2
            if j > 0:
                self.dma("sp", self.kp[kb][:, 0:nprev * 128], self.kc_d[h, :, 0:nprev * 128], [("kc", jj) for jj in range(j)],
                         [("kp", kb)], key=("kp", kb))
                self.dma("sp", self.vp[kb][:, 0:nprev, :], self.vc_d[0:nprev * 128, h * 128:(h + 1) * 128].rearrange("(k p) d -> p k d", p=128),
                         [("vc", jj) for jj in range(j)], [("vp", kb)], key=("vp", kb))

            def kv(kt):
                if kt < nprev:
                    K = self.kp[kb][:, kt * 128:(kt + 1) * 128]
                    V = self.vp[kb][:, kt, :]
                    return K, V, [("kp", kb)], [("vp", kb)]
                a = kt - nprev
                return self.KTc[:, h, a * 128:(a + 1) * 128], self.Vc[:, a, h * 128:(h + 1) * 128], [("KTc", h)], [("Vc", a)]

            def smm(kt):
                K, V, rk, rv = kv(kt)
                sbk = kt % 2
                self.mm(self.PS[2 * sbk][:, :], K[0:64, :], self.QT[0:64, h, :], True, True, rk + [("QT", h)], [("B", 2 * sbk)])
                self.mm(self.PS[2 * sbk + 1][:, :], K[64:128, :], self.QT[64:128, h, :], True, True, rk + [("QT", h)], [("B", 2 * sbk + 1)])

            smm(0)
            for kt in range(nkt):
                if kt + 1 < nkt:
                    smm(kt + 1)
                if kt == 0 and post is not None:
                    post[0]()
                K, V, rk, rv = kv(kt)
                sbk = kt % 2
                pt = self.PTA[:, sbk, :, :]
                rp = ("PT", sbk)
                self.act(pt, self.PSA[:, 2 * sbk * TT:(2 * sbk + 2) * TT].rearrange("p (m t) -> p m t", m=2), AF.Exp,
                         [("B", 2 * sbk), ("B", 2 * sbk + 1)], [rp])
                if post is not None and kt in sched:
                    post[sched[kt]]()
                    if sched[kt] == 3:
                        post = None
                if kt >= nprev:
                    a = kt - nprev
                    self.tt("dve", pt, pt, self.maskA[:, a:a + 1, :].to_broadcast([128, 2, TT]), ALU.mult, [rp, "maskA"], [rp])
                p1, p2 = self.PTA[:, sbk, 0, :], self.PTA[:, sbk, 1, :]
                first, last = kt == 0, kt == nkt - 1
                acc, racc = self.acc_buf(h)
                if first:
                    self.copy("dve", acc, p1, [rp], racc)
                else:
                    self.tt("dve", acc, acc, p1, ALU.add, racc + [rp], racc)
                self.mm(self.PS[4][:, :], V, p1, first, last, rv + [rp], [("B", 4)])
                self.mm(self.PS[5][:, :], V, p2, first, last, rv + [rp], [("B", 5)])
                self.mm(self.PS[7][:, :], self.ones_b[:, :], p2, first, last, ["ones_b", rp], [("B", 7)])
            post = self.attn_post(h)
        for kt in (0, 1, 2, 3):
            post[kt]()

    def acc_buf(self, h):
        if h % 2 == 0:
            return self.SF[5][:, :], [("S", 5)]
        return self.arena[:, 20:22, :].bitcast(F32).rearrange("p a b -> p (a b)"), [("A", 20), ("A", 21)]

    def attn_post(self, h):
        ra, rb, t1, t2 = self.SF[0], self.SF[1], self.SF[2], self.SF[3]
        o = self.SF[4]
        osq = self.SBb[0]
        acc, racc = self.acc_buf(h)

        def s0():
            pass
        self.copy("dve", t1[:, :], self.PS[4][:, :], [("B", 4)], [("S", 2)])
        self.copy("dve", t2[:, :], self.PS[5][:, :], [("B", 5)], [("S", 3)])
        self.act(rb[:, :], self.PS[7][:, :], AF.Ln, [("B", 7)], [("S", 1)])

        def s1():
            self.mm(self.PS[6][:, :], self.ones_f[:, :], acc, True, True, ["ones_f"] + racc, [("B", 6)])
            self.act(ra[:, :], self.PS[6][:, :], AF.Ln, [("B", 6)], [("S", 0)])
            self.act(ra[:, :], ra[:, :], AF.Exp, [("S", 0)], [("S", 0)], scale=-1.0)
            self.act(rb[:, :], rb[:, :], AF.Exp, [("S", 1)], [("S", 1)], scale=-1.0)
            self.tt("dve", t1[:, :], t1[:, :], ra[:, :], ALU.mult, [("S", 2), ("S", 0)], [("S", 2)])
            self.tt("dve", t2[:, :], t2[:, :], rb[:, :], ALU.mult, [("S", 3), ("S", 1)], [("S", 3)])
            self.stt(o[:, :], t2[:, :], self.cc(self.C_NEGLAM), t1[:, :], ALU.mult, ALU.add, [("S", 2), ("S", 3), "c_neglam"], [("S", 4)])
            self.tt("dve", osq[:, :], o[:, :], o[:, :], ALU.mult, [("S", 4)], [("SB", 0)])

        def s2():
            self.mm(self.PS[0][:, :], self.ones_b[:, :], osq[:, :], True, True, ["ones_b", ("SB", 0)], [("B", 0)])
            self.act(self.SF[6][:, :], self.PS[0][:, :], AF.Ln, [("B", 0), "c_eps"], [("S", 6)], bias=self.cc(self.C_EPS), scale=1.0 / 128)

        def s3():
            self.act(self.SF[7][:, :], self.SF[6][:, :], AF.Exp, [("S", 6)], [("S", 7)], scale=-0.5)
            self.stt(self.ybT[:, h, :], o[:, :], self.cc(self.C_GSUB), self.SF[7][:, :], ALU.mult, ALU.mult,
                     [("S", 4), ("S", 7), "c_subln"], [("ybT", h)])
        return {0: s0, 1: s1, 2: s2, 3: s3}

    def mixer_out(self):
        pav, rpa = self.wslab("pa", 0, 1024)
        m = self.hT
        pbv = [None, None]
        wov = [None, None]
        for half in range(2):
            pbv[half] = self.wslab("pb", half * 512, 512)
            for ci in range(4):
                c = half * 4 + ci
                ba, bb = (0, 2) if c % 2 == 0 else (1, 3)
                for kc in range(4):
                    self.mm(self.PS[ba][:, :], pav[:, kc, c * 128:(c + 1) * 128], self.yaT[:, kc, :], kc == 0, kc == 3,
                            [rpa, ("yaT", kc)], [("B", ba)])
                sv, rsl = pbv[half]
                for kc in range(8):
                    self.mm(self.PS[bb][:, :], sv[:, kc, ci * 128:(ci + 1) * 128], self.ybT[:, kc, :], kc == 0, kc == 7,
                            [rsl, ("ybT", kc)], [("B", bb)])
                ga, rga = self.sgA(c)
                gb, rgb = self.sgB(c)
                ta, tb_ = self.SF[0 + c % 2], self.SF[2 + c % 2]
                self.tt("dve", ta[:, :], self.PS[ba][:, :], ga, ALU.mult, [("B", ba), rga], [("S", 0 + c % 2)])
                self.tt("dve", tb_[:, :], self.PS[bb][:, :], gb, ALU.mult, [("B", bb), rgb], [("S", 2 + c % 2)])
                self.tt("dve", m[:, c, :], ta[:, :], tb_[:, :], ALU.add, [("S", 0 + c % 2), ("S", 2 + c % 2)], [("hT", c)])
        for half in range(2):
            sv, rsl = self.wslab("wo", half * 512, 512)
            for ci in range(4):
                c = half * 4 + ci
                b = 4 + c % 2
                for kc in range(8):
                    self.mm(self.PS[b][:, :], sv[:, kc, ci * 128:(ci + 1) * 128], m[:, kc, :], kc == 0, kc == 7,
                            [rsl, ("hT", kc)], [("B", b)])
                if c > 0:
                    self.sumsq_mm(c - 1)
                self.tt("dve", self.xT[:, c, :], self.PS[b][:, :], self.xT[:, c, :], ALU.add, [("B", b), ("xT", c)], [("xT", c)])
                self.sumsq_sq(c)
        self.sumsq_mm(7)

    def build(self, stages=("ffn1", "mix", "ffn2")):
        self.setup()
        self.load_x_dma(0)
        for j in range(self.nt):
            self.load_x_tr(j)
            if "ffn1" in stages:
                self.ffn(1)
            if j == 0:
                self.dbg("x1", self.xT[:, :, :], [128, 8, TT], [("xT", c) for c in range(8)])
            if "mix" in stages:
                self.mixer_in(j)
                if j == 0:
                    self.dbg("QT", self.QT[:, :, :], [128, 8, TT], [("QT", h) for h in range(8)], BF16)
                    self.dbg("KT", self.KTc[:, :, :], [128, 8, TT], [("KTc", h) for h in range(8)], BF16)
                    self.dbg("V", self.Vc[:, :, :], [128, 4, D], [("Vc", s) for s in range(4)], BF16)
                    self.dbg("vn", self.vn[:, :, :], [128, 4, 512], [("vn", s) for s in range(4)], BF16)
                    self.dbg("gu", self.arena[:, 0:20, :], [128, 20, TT], [("A", c) for c in range(20)], BF16)
                self.gmlp()
                self.attention(j)
                if j == 0:
                    self.dbg("yaT", self.yaT[:, :, :], [128, 4, TT], [("yaT", c) for c in range(4)], BF16)
                    self.dbg("ybT", self.ybT[:, :, :], [128, 8, TT], [("ybT", c) for c in range(8)], BF16)
                self.mixer_out()
                if j + 1 < self.nt:
                    self.load_x_dma(j + 1)
                if j == 0:
                    self.dbg("x2", self.xT[:, :, :], [128, 8, TT], [("xT", c) for c in range(8)])
            if "ffn2" in stages:
                self.ffn(2)
            self.store_x(j)
        self.P.add("sp", None, ["OUT"], ())
        self.P.finalize()
        self.emit()
        return self.nc

    def emit(self):
        nc, P = self.nc, self.P
        with ExitStack() as es:
            sems = {}
            for i, k in enumerate(P.sem_keys()):
                sems[k] = es.enter_context(nc.semaphore(f"s{i}"))
            block = es.enter_context(nc.Block())
            block.tensor(lambda e: P.emit_engine("pe", e, sems))
            block.scalar(lambda e: P.emit_engine("act", e, sems))
            block.vector(lambda e: P.emit_engine("dve", e, sems))
            block.gpsimd(lambda e: P.emit_engine("pool", e, sems))
            block.sync(lambda e: P.emit_engine("sp", e, sems))


def prep_common(inp):
    f = lambda k: np.ascontiguousarray(np.asarray(inp[k], dtype=np.float32)[0])
    c = {
        "wg1": f("ffn1_w_gate"), "wu1": f("ffn1_w_up"), "wd1": f("ffn1_w_down"),
        "win": f("w_in"), "pa": f("a_w_proj"), "pb": f("b_w_proj"), "wo": f("w_out"),
        "wg2": f("ffn2_w_gate"), "wu2": f("ffn2_w_up"), "wd2": f("ffn2_w_down"),
    }
    col = lambda k: np.ascontiguousarray(f(k).reshape(8, 128).T)
    c["g1"], c["gm"], c["g2"] = col("ffn1_norm"), col("mix_norm"), col("ffn2_norm")
    bc = lambda v: np.ascontiguousarray(np.broadcast_to(v[None, :], (128, v.shape[0])))
    c["gq_t"] = bc(f("b_q_norm")[:16])
    c["gk_t"] = bc(f("b_k_norm")[:16])
    c["gqk_col"] = np.ascontiguousarray(np.stack([np.tile(f("b_q_norm"), 2), np.tile(f("b_k_norm"), 2)], axis=1))
    c["lng_t"] = bc(f("a_ln_gain"))
    c["lnb_t"] = bc(f("a_ln_bias"))
    bs = f("a_b_s")
    bsT = np.empty((128, 4, 128), np.float32)
    for p in range(128):
        for gp in range(4):
            bsT[p, gp, :] = bs[2 * gp + p // 64, :]
    c["bsT"] = bsT.reshape(128, 512)
    c["ws"] = f("a_w_s")
    lam = np.concatenate([f("b_lambda_q1"), f("b_lambda_k1"), f("b_lambda_q2"), f("b_lambda_k2")])
    c["lamv"] = bc(lam)
    c["subln"] = np.ascontiguousarray(f("b_subln").reshape(128, 1))
    return c


_NC_CACHE = {}


def run(inputs, debug=(), stages=("ffn1", "mix", "ffn2"), cores=None):
    x = np.asarray(inputs["x"], dtype=np.float32)
    pos = np.asarray(inputs["positions"]).astype(np.int32)
    B, S, _ = x.shape
    nt = S // TT
    key = (nt, tuple(debug), tuple(stages))
    if key not in _NC_CACHE:
        _NC_CACHE[key] = Builder(nt, debug)
        _NC_CACHE[key].build(stages)
    bld = _NC_CACHE[key]
    common = prep_common(inputs)
    in_maps = []
    for b in range(B):
        m = dict(common)
        m["x"] = np.ascontiguousarray(x[b])
        m["pos_t"] = np.ascontiguousarray(pos[b].reshape(S // 128, 128).T)
        in_maps.append(m)
    res = run_bass_kernel_spmd(bld.nc, in_maps, core_ids=list(range(B)))
    return res


def kernel(**inputs):
    res = run(inputs)
    return np.stack([np.asarray(r["out"], dtype=np.float32) for r in res.results], axis=0)
```

```python
import math
from contextlib import ExitStack

import numpy as np
import concourse.bass as bass
import concourse.mybir as mybir
from concourse.bass_utils import run_bass_kernel_spmd

F32 = mybir.dt.float32
BF16 = mybir.dt.bfloat16
I32 = mybir.dt.int32
ALU = mybir.AluOpType
AF = mybir.ActivationFunctionType
AX = mybir.AxisListType

D = 1024
DFF = 2816
NFC = DFF // 128
TT = 512
NORM_EPS = 1e-6
LN_EPS = 1e-5
LAM_INIT = 0.8 - 0.6 * math.exp(-0.3 * 0)
ROPE_THETA = 500000.0
NSLAB = 4


class Op:
    __slots__ = ("eng", "fn", "dma_key", "group", "idx", "deps", "dma_n", "signals", "sigval")


class Prog:
    def __init__(self):
        self.ops = []
        self.lastw = {}
        self.rd_eng = {}
        self.rd_dma = {}
        self.dma_count = {}
        self.waitall = set()

    def add(self, eng, fn, reads=(), writes=(), dma_key=None, group=False, waitall=False):
        op = Op()
        op.eng, op.fn, op.dma_key, op.group = eng, fn, dma_key, group
        op.idx = len(self.ops)
        op.deps = {}
        op.dma_n = 0

        def dep(p, kind):
            old = op.deps.get(p)
            if old is None or (old == "war" and kind != "war"):
                op.deps[p] = kind

        for r in reads:
            for w in self.lastw.get(r, ()):
                dep(w, "raw")
        for r in writes:
            for w in self.lastw.get(r, ()):
                if not (group and w.group):
                    dep(w, "waw")
            for p in self.rd_eng.get(r, {}).values():
                dep(p, "war")
            for p in self.rd_dma.get(r, ()):
                dep(p, "war")
        for r in reads:
            if dma_key is not None:
                self.rd_dma.setdefault(r, []).append(op)
            else:
                self.rd_eng.setdefault(r, {})[eng] = op
        for r in writes:
            lw = self.lastw.get(r)
            if group and lw and all(w.group for w in lw):
                lw.append(op)
            else:
                self.lastw[r] = [op]
                self.rd_eng[r] = {}
                self.rd_dma[r] = []
        if dma_key is not None:
            self.dma_count[dma_key] = self.dma_count.get(dma_key, 0) + 1
            op.dma_n = self.dma_count[dma_key]
            if waitall:
                self.waitall.add(dma_key)
        op.signals = dma_key is not None
        self.ops.append(op)
        return op

    @staticmethod
    def _skip(p, op, kind):
        return p.dma_key is None and op.dma_key is None and p.eng == op.eng and p.eng == "pe"

    def finalize(self):
        for op in self.ops:
            for p, kind in op.deps.items():
                if p.dma_key is None and not self._skip(p, op, kind):
                    p.signals = True
        cnt = {}
        for op in self.ops:
            if op.dma_key is None:
                if op.signals:
                    cnt[op.eng] = cnt.get(op.eng, 0) + 1
                op.sigval = cnt.get(op.eng, 0)
            else:
                n = self.dma_count[op.dma_key] if op.dma_key in self.waitall else op.dma_n
                op.sigval = 16 * n

    def sem_keys(self):
        keys = []
        seen = set()
        for op in self.ops:
            k = ("d", op.dma_key) if op.dma_key is not None else ("e", op.eng)
            if k not in seen:
                seen.add(k)
                keys.append(k)
        return keys

    def emit_engine(self, name, eng, sems):
        waited = {}
        for op in self.ops:
            if op.eng != name:
                continue
            needs = {}
            for p, kind in op.deps.items():
                if self._skip(p, op, kind):
                    continue
                key = ("d", p.dma_key) if p.dma_key is not None else ("e", p.eng)
                if needs.get(key, 0) < p.sigval:
                    needs[key] = p.sigval
            for key, v in needs.items():
                if waited.get(key, 0) < v:
                    eng.wait_ge(sems[key], v)
                    waited[key] = v
            if op.fn is None:
                continue
            ins = op.fn(eng)
            if op.dma_key is not None:
                ins.then_inc(sems[("d", op.dma_key)], 16)
            elif op.signals:
                ins.then_inc(sems[("e", name)], 1)


class Builder:
    def __init__(self, nt, debug=()):
        self.nt = nt
        self.S = nt * TT
        self.debug = set(debug)
        self.P = Prog()
        self.nc = bass.Bass("TRN2", target_bir_lowering=False)
        self.slab_ctr = 0
        self.dbg_outs = {}
        self._alloc()

    def din(self, name, shape, dt=F32):
        return self.nc.dram_tensor(name, list(shape), dt, kind="ExternalInput").ap()

    def dscratch(self, name, shape, dt=BF16):
        return self.nc.dram_tensor(name, list(shape), dt).ap()

    def sb(self, name, shape, dt):
        return self.nc.alloc_sbuf_tensor(name, list(shape), dt)

    def _alloc(self):
        nc, S, nt = self.nc, self.S, self.nt
        self.x_d = self.din("x", [S, D])
        self.out_d = nc.dram_tensor("out", [S, D], F32, kind="ExternalOutput").ap()
        self.pos_d = self.din("pos_t", [128, nt * 4], I32)
        wshapes = {
            "wg1": (D, DFF), "wu1": (D, DFF), "wd1": (DFF, D),
            "win": (D, 6144), "pa": (512, D), "pb": (D, D), "wo": (D, D),
            "wg2": (D, DFF), "wu2": (D, DFF), "wd2": (DFF, D),
        }
        self.w_f32 = {k: self.din(k, v) for k, v in wshapes.items()}
        self.w_bf = {k: self.dscratch(k + "_bf", v) for k, v in wshapes.items()}
        self.wshapes = wshapes
        self.cin = {
            "g1": self.din("g1", [128, 8]), "gm": self.din("gm", [128, 8]), "g2": self.din("g2", [128, 8]),
            "gq_t": self.din("gq_t", [128, 16]), "gk_t": self.din("gk_t", [128, 16]), "gqk_col": self.din("gqk_col", [128, 2]),
            "lng_t": self.din("lng_t", [128, 512]), "lnb_t": self.din("lnb_t", [128, 512]),
            "bsT": self.din("bsT", [128, 512]), "ws": self.din("ws", [8, 128, 128]),
            "lamv": self.din("lamv", [128, 256]), "subln": self.din("subln", [128, 1]),
        }
        self.kc_d = self.dscratch("kcache", [8, 128, S])
        self.vc_d = self.dscratch("vcache", [S, D])
        self.xT = self.sb("xT", [128, 8, TT], F32)
        self.hT = self.sb("hT", [128, 8, TT], BF16)
        self.arena = self.sb("arena", [128, NFC, TT], BF16)
        self.yaT = self.sb("yaT", [128, 4, TT], BF16)
        self.ybT = self.sb("ybT", [128, 8, TT], BF16)
        self.QT = self.sb("QT", [128, 8, TT], BF16)
        self.KTc = self.sb("KTc", [128, 8, TT], BF16)
        self.Vc = self.sb("Vc", [128, 4, D], BF16)
        self.vn = self.sb("vn", [128, 4, 512], BF16)
        self.SL = [self.sb(f"slab{i}", [128, 4096], BF16) for i in range(NSLAB)]
        self.kp = [self.sb(f"kp{i}", [128, max(S - TT, 128)], BF16) for i in range(2)]
        self.vp = [self.sb(f"vp{i}", [128, max(nt * 4 - 4, 1), 128], BF16) for i in range(2)]
        self.PTA = self.sb("pta", [128, 2, 2, TT], BF16)
        self.stg = [self.sb(f"stg{i}", [128, D], F32) for i in range(2)]
        self.SF = [self.sb(f"sf{i}", [128, TT], F32) for i in range(8)]
        self.cosT = self.SF[6][:, 0:nt * 64].rearrange("p (s e) -> p s e", e=16)
        self.sinT = self.SF[7][:, 0:nt * 64].rearrange("p (s e) -> p s e", e=16)
        self.SBb = [self.sb(f"sbb{i}", [128, TT], BF16) for i in range(2)]
        self.sm = self.sb("sm", [128, 256], F32)
        self.rt = [self.sb(f"rt{i}", [128, 128], F32) for i in range(8)]
        self.ident = self.sb("ident", [128, 128], F32)
        self.ones_f = self.sb("ones_f", [128, 128], F32)
        self.tri = self.sb("tri", [128, 128], F32)
        self.ones_b = self.sb("ones_b", [128, 128], BF16)
        self.onesbig = self.sb("onesbig", [128, TT], BF16)
        self.maskA = self.sb("maskA", [128, 4, TT], BF16)
        self.cg = {k: self.sb("c_" + k, [128, 8], F32) for k in ("g1", "gm", "g2")}
        self.gq_t = self.sb("gq_ts", [128, 16], F32)
        self.gk_t = self.sb("gk_ts", [128, 16], F32)
        self.gcol = self.sb("gcol", [128, 2], F32)
        self.ropeT = {k: self.sb("rope_" + k, [128, nt * 4, 32], F32) for k in "qk"}
        self.lng_t = self.sb("lng_ts", [128, 512], F32)
        self.lnb_t = self.sb("lnb_ts", [128, 512], F32)
        self.bsT = self.sb("bsTs", [128, 512], F32)
        self.wsT = self.sb("wsT", [128, 8, 128], BF16)
        self.lamv = self.sb("lamvs", [128, 256], F32)
        self.csm = self.sb("csm", [128, 16], F32)
        self.pos_i = self.sb("pos_i", [128, nt * 4], I32)
        self.rpi = self.sb("rpi", [128, nt * 4, 8], I32)
        self.PSA = nc.alloc_psum_tensor("psall", [128, 8 * TT], F32)
        self.PS = [self.PSA[:, i * TT:(i + 1) * TT] for i in range(8)]

    C_EPS, C_EPSLN, C_NEGLAM, C_GSUB, C_T0, C_T1, C_T2, C_T3, C_HALFPI = range(9)

    def cc(self, i):
        return self.csm[:, i:i + 1]

    def mm(self, out, lhsT, rhs, start, stop, reads, writes):
        self.P.add("pe", lambda e: e.matmul(out, lhsT, rhs, start=start, stop=stop), reads, writes)

    def tr(self, out, in_, reads, writes):
        ident = self.ident[:, :]
        self.P.add("pe", lambda e: e.transpose(out, in_, ident), list(reads) + ["ident"], writes)

    def act(self, out, in_, func, reads, writes, bias=None, scale=None):
        kw = {}
        if bias is not None:
            kw["bias"] = bias
        if scale is not None:
            kw["scale"] = scale
        self.P.add("act", lambda e: e.activation(out=out, in_=in_, func=func, **kw), reads, writes)

    def tt(self, eng, out, in0, in1, op, reads, writes):
        self.P.add(eng, lambda e: e.tensor_tensor(out=out, in0=in0, in1=in1, op=op), reads, writes)

    def ts(self, eng, out, in0, s1, s2, op0, op1, reads, writes):
        if op1 is None:
            self.P.add(eng, lambda e: e.tensor_scalar(out=out, in0=in0, scalar1=s1, scalar2=None, op0=op0), reads, writes)
        else:
            self.P.add(eng, lambda e: e.tensor_scalar(out=out, in0=in0, scalar1=s1, scalar2=s2, op0=op0, op1=op1), reads, writes)

    def stt(self, out, in0, scalar, in1, op0, op1, reads, writes):
        self.P.add("dve", lambda e: e.scalar_tensor_tensor(out=out, in0=in0, scalar=scalar, in1=in1, op0=op0, op1=op1), reads, writes)

    def copy(self, eng, out, in_, reads, writes):
        if eng == "act":
            self.P.add("act", lambda e: e.copy(out=out, in_=in_), reads, writes)
        else:
            self.P.add(eng, lambda e: e.tensor_copy(out=out, in_=in_), reads, writes)

    def rsqrt_act(self, out, in_, scale, eps_ap, eps_res, tmp, reads, rtmp, rout, power=-0.5):
        if eps_ap is None:
            self.act(tmp, in_, AF.Ln, reads, [rtmp], scale=scale)
        else:
            self.act(tmp, in_, AF.Ln, list(reads) + [eps_res], [rtmp], bias=eps_ap, scale=scale)
        self.act(out, tmp, AF.Exp, [rtmp], [rout], scale=power)

    def recip(self, out, in_, reads, writes):
        self.P.add("dve", lambda e: e.reciprocal(out=out, in_=in_), reads, writes)

    def memset(self, eng, ap, val, writes):
        self.P.add(eng, lambda e: e.memset(ap, val), (), writes)

    def dma(self, q, out, in_, reads, writes, key, group=False, waitall=False):
        self.P.add(q, lambda e: e.dma_start(out=out, in_=in_), reads, writes, dma_key=key, group=group, waitall=waitall)

    def dbg(self, name, ap, shape, reads, dt=F32):
        if name not in self.debug:
            return
        d = self.nc.dram_tensor("dbg_" + name, list(shape), dt, kind="ExternalOutput").ap()
        self.dbg_outs[name] = d
        self.dma("sp", d, ap, reads, ["OUT"], key=("dbg", name), group=True)

    def setup(self):
        nt = self.nt
        cl = [("g1", self.cg["g1"]), ("gm", self.cg["gm"]), ("g2", self.cg["g2"]), ("gq_t", self.gq_t), ("gk_t", self.gk_t),
              ("lng_t", self.lng_t), ("lnb_t", self.lnb_t), ("bsT", self.bsT), ("lamv", self.lamv), ("gqk_col", self.gcol)]
        for k, t in cl:
            self.dma("sp", t[:, :], self.cin[k], (), ["c_" + k], key="consts", group=True, waitall=True)
        self.dma("sp", self.csm[:, self.C_GSUB:self.C_GSUB + 1], self.cin["subln"], (), ["c_subln"], key="consts", group=True, waitall=True)
        self.dma("sp", self.pos_i[:, :], self.pos_d, (), ["c_pos"], key="consts", group=True, waitall=True)
        for half in range(2):
            dst = self.SF[half][:, :].rearrange("p (g s) -> p g s", g=4)
            src = self.cin["ws"][half * 4:(half + 1) * 4, :, :].rearrange("g t s -> t g s")
            self.dma("sp", dst, src, (), [("S", half)], key=("wsld", half))
        self.memset("pool", self.ones_f[:, :], 1.0, ["ones_f"])
        self.memset("pool", self.ones_b[:, :], 1.0, ["ones_b"])
        self.memset("pool", self.onesbig[:, :], 1.0, ["onesbig"])
        ones_f, ident, tri = self.ones_f[:, :], self.ident[:, :], self.tri[:, :]
        self.P.add("pool", lambda e: e.affine_select(out=ident, in_=ones_f, pattern=[[1, 128]], compare_op=ALU.is_equal,
                                                     fill=0.0, base=0, channel_multiplier=-1), ["ones_f"], ["ident"])
        self.P.add("pool", lambda e: e.affine_select(out=tri, in_=ones_f, pattern=[[1, 128]], compare_op=ALU.is_ge,
                                                     fill=0.0, base=0, channel_multiplier=-1), ["ones_f"], ["tri"])
        for a in range(4):
            self._mask(a)
        def conv(name, gi, c0, c1):
            self.dma("pool", self.w_bf[name][:, c0:c1], self.w_f32[name][:, c0:c1], (), [("wbf", name, gi)], key=("cv", name, gi))
        ffn_groups = [(0, 4), (4, 4), (8, 4), (12, 4), (16, 4), (20, 2)]
        def conv_ffn(which):
            for gi, (j0, nj) in enumerate(ffn_groups):
                conv(f"wg{which}", gi, j0 * 128, (j0 + nj) * 128)
                conv(f"wu{which}", gi, j0 * 128, (j0 + nj) * 128)
            for gi in range(2):
                conv(f"wd{which}", gi, gi * 512, (gi + 1) * 512)
        conv_ffn(1)
        for gi in range(12):
            conv("win", gi, gi * 512, (gi + 1) * 512)
        conv("pa", 0, 0, 1024)
        for gi in range(2):
            conv("pb", gi, gi * 512, (gi + 1) * 512)
        for gi in range(2):
            conv("wo", gi, gi * 512, (gi + 1) * 512)
        conv_ffn(2)
        self.memset("dve", self.cc(self.C_EPS), NORM_EPS, ["c_eps"])
        self.memset("dve", self.cc(self.C_EPSLN), LN_EPS, ["c_epsln"])
        self.memset("dve", self.cc(self.C_HALFPI), math.pi / 2, ["c_halfpi"])
        self.ts("dve", self.gq_t[:, :], self.gq_t[:, :], 0.125, None, ALU.mult, None, ["c_gq_t"], ["c_gq_t"])
        self.ts("dve", self.gcol[:, 0:1], self.gcol[:, 0:1], 0.125, None, ALU.mult, None, ["c_gqk_col"], ["c_gqk_col"])
        for p0 in (0, 64):
            self.memset("dve", self.gcol[p0:p0 + 16, :], 1.0, ["c_gqk_col"])
        self.ts("dve", self.cc(self.C_GSUB), self.cc(self.C_GSUB), 1.0 - LAM_INIT, None, ALU.mult, None, ["c_subln"], ["c_subln"])
        lv = self.lamv
        self.tt("dve", self.sm[:, 0:64], lv[:, 0:64], lv[:, 64:128], ALU.mult, ["c_lamv"], [("sm", 0)])
        t0 = self.cc(self.C_T0)
        self.P.add("dve", lambda e: e.tensor_reduce(out=t0, in_=self.sm[:, 0:64], axis=AX.X, op=ALU.add), [("sm", 0)], ["c_t0"])
        self.tt("dve", self.sm[:, 0:64], lv[:, 128:192], lv[:, 192:256], ALU.mult, ["c_lamv", "c_t0"], [("sm", 0)])
        t1 = self.cc(self.C_T1)
        self.P.add("dve", lambda e: e.tensor_reduce(out=t1, in_=self.sm[:, 0:64], axis=AX.X, op=ALU.add), [("sm", 0)], ["c_t1"])
        self.act(self.cc(self.C_T2), t0, AF.Exp, ["c_t0"], ["c_t2"])
        self.act(self.cc(self.C_T3), t1, AF.Exp, ["c_t1"], ["c_t3"])
        self.tt("dve", self.cc(self.C_NEGLAM), self.cc(self.C_T3), self.cc(self.C_T2), ALU.subtract, ["c_t2", "c_t3"], ["c_neglam"])
        self.ts("dve", self.cc(self.C_NEGLAM), self.cc(self.C_NEGLAM), -LAM_INIT, None, ALU.add, None, ["c_neglam"], ["c_neglam"])
        for g in range(8):
            half, gi = g // 4, g % 4
            b = g // 4
            self.tr(self.PS[b][:, gi * 128:(gi + 1) * 128], self.SF[half][:, gi * 128:(gi + 1) * 128], [("S", half)], [("B", b)])
            self.tt("dve", self.wsT[:, g, :], self.PS[b][:, gi * 128:(gi + 1) * 128], self.tri[:, :], ALU.mult, [("B", b), "tri"], ["wsT"])
        self._rope_tables()
        self._rope_gain_tables()

    def _mask(self, a):
        out = self.maskA[:, a, :]
        ob = self.onesbig[:, :]
        self.P.add("pool", lambda e: e.affine_select(out=out, in_=ob, pattern=[[1, TT]], compare_op=ALU.is_ge, fill=0.0,
                                                     base=-128 * a, channel_multiplier=-1), ["onesbig"], ["maskA"])

    def _rope_tables(self):
        nst = self.nt * 4
        inv = (np.float32(ROPE_THETA) ** (-np.arange(0, 16, 2, dtype=np.float32) / np.float32(16))).astype(np.float32)
        ang, kf, r = [self.SF[2 + i][:, 0:nst * 8].rearrange("p (s e) -> p s e", e=8) for i in range(3)]
        posf = self.sm[:, 0:nst]
        self.copy("dve", posf, self.pos_i[:, :], ["c_pos", ("sm", 0)], [("sm", 0)])
        for i in range(8):
            self.ts("dve", ang[:, :, i], posf, float(inv[i]), None, ALU.mult, None, [("sm", 0)], [("S", 2)])
        C1 = 6.28125
        C2 = 2 * math.pi - C1
        self.ts("dve", kf[:, :, :], ang[:, :, :], 1.0 / (2 * math.pi), None, ALU.mult, None, [("S", 2)], [("S", 3)])
        self.copy("dve", self.rpi[:, :, :], kf[:, :, :], [("S", 3)], ["rpi"])
        self.copy("dve", kf[:, :, :], self.rpi[:, :, :], ["rpi"], [("S", 3)])
        self.stt(r[:, :, :], kf[:, :, :], -C1, ang[:, :, :], ALU.mult, ALU.add, [("S", 2), ("S", 3)], [("S", 4)])
        self.stt(r[:, :, :], kf[:, :, :], -C2, r[:, :, :], ALU.mult, ALU.add, [("S", 3), ("S", 4)], [("S", 4)])
        self.ts("dve", kf[:, :, :], r[:, :, :], math.pi, -2 * math.pi, ALU.is_gt, ALU.mult, [("S", 4)], [("S", 3)])
        self.tt("dve", r[:, :, :], r[:, :, :], kf[:, :, :], ALU.add, [("S", 3), ("S", 4)], [("S", 4)])
        self.ts("dve", kf[:, :, :], r[:, :, :], -math.pi, 2 * math.pi, ALU.is_lt, ALU.mult, [("S", 4)], [("S", 3)])
        self.tt("dve", r[:, :, :], r[:, :, :], kf[:, :, :], ALU.add, [("S", 3), ("S", 4)], [("S", 4)])
        self.ts("dve", r[:, :, :], r[:, :, :], 3.1415925, -3.1415925, ALU.min, ALU.max, [("S", 4)], [("S", 4)])
        self.act(self.sinT[:, :, 8:16], r[:, :, :], AF.Sin, [("S", 4)], [("S", 7)])
        self.act(self.sinT[:, :, 0:8], r[:, :, :], AF.Sin, [("S", 4)], [("S", 7)], scale=-1.0)
        self.act(ang[:, :, :], r[:, :, :], AF.Abs, [("S", 4), ("S", 2)], [("S", 2)])
        self.act(self.cosT[:, :, 0:8], ang[:, :, :], AF.Sin, [("S", 2), "c_halfpi"], [("S", 6)], bias=self.cc(self.C_HALFPI), scale=-1.0)
        self.act(self.cosT[:, :, 8:16], ang[:, :, :], AF.Sin, [("S", 2), "c_halfpi"], [("S", 6)], bias=self.cc(self.C_HALFPI), scale=-1.0)

    def _rope_gain_tables(self):
        nst = self.nt * 4
        for k, gt, gres in (("q", self.gq_t, "c_gq_t"), ("k", self.gk_t, "c_gk_t")):
            T = self.ropeT[k]
            g1 = gt[:, 0:8].unsqueeze(1).to_broadcast([128, nst, 8])
            g2 = gt[:, 8:16].unsqueeze(1).to_broadcast([128, nst, 8])
            rr = [("S", 6), ("S", 7), gres]
            self.tt("dve", T[:, :, 0:8], self.cosT[:, :, 0:8], g1, ALU.mult, rr, ["ropeT" + k])
            self.tt("dve", T[:, :, 8:16], self.cosT[:, :, 8:16], g2, ALU.mult, rr, ["ropeT" + k])
            self.tt("dve", T[:, :, 16:24], self.sinT[:, :, 0:8], g2, ALU.mult, rr, ["ropeT" + k])
            self.tt("dve", T[:, :, 24:32], self.sinT[:, :, 8:16], g1, ALU.mult, rr, ["ropeT" + k])

    def slab(self, wname, gi, src, shape3):
        i = self.slab_ctr % NSLAB
        self.slab_ctr += 1
        a, b = shape3
        view = self.SL[i][:, 0:a * b].rearrange("p (a b) -> p a b", a=a)
        self.dma("sp", view, src, [("wbf", wname, gi)], [("slab", i)], key=("slab", i))
        return view, ("slab", i)

    def wslab(self, wname, c0, ncols):
        w = self.w_bf[wname]
        kc = self.wshapes[wname][0] // 128
        src = w[:, c0:c0 + ncols].rearrange("(c p) f -> p c f", p=128)
        return self.slab(wname, 0 if wname == "pa" else c0 // 512, src, (kc, ncols))

    def xstage(self, s):
        t, nm = (self.QT, "QT") if s < 2 else (self.KTc, "KTc")
        h0 = (s % 2) * 4
        view = t[:, h0:h0 + 4, :].bitcast(F32).rearrange("p a b -> p (a b)")
        return view, [(nm, h) for h in range(h0, h0 + 4)]

    def load_x_dma(self, j):
        for s in range(4):
            view, res = self.xstage(s)
            r0 = j * TT + s * 128
            self.dma("sp", view, self.x_d[r0:r0 + 128, :], (), res, key=("xld", s))

    def load_x_tr(self, j):
        for s in range(4):
            view, res = self.xstage(s)
            for half in range(2):
                b = half
                for ci in range(4):
                    c = half * 4 + ci
                    self.tr(self.PS[b][:, ci * 128:(ci + 1) * 128], view[:, c * 128:(c + 1) * 128], res, [("B", b)])
                self.copy("act", self.xT[:, half * 4:(half + 1) * 4, s * 128:(s + 1) * 128],
                          self.PS[b][:, :].rearrange("p (c t) -> p c t", c=4), [("B", b)], [("xT", c) for c in range(half * 4, half * 4 + 4)])

    def store_x(self, j):
        for s in range(4):
            st = self.stg[s % 2]
            for half in range(2):
                b = 6 + half
                for ci in range(4):
                    c = half * 4 + ci
                    self.tr(self.PS[b][:, ci * 128:(ci + 1) * 128], self.xT[:, c, s * 128:(s + 1) * 128], [("xT", c)], [("B", b)])
                self.copy("act", st[:, half * 512:(half + 1) * 512], self.PS[b][:, :], [("B", b)], [("stg", s % 2)])
            r0 = j * TT + s * 128
            self.dma("sp", self.out_d[r0:r0 + 128, :], st[:, :], [("stg", s % 2)], ["OUT"], key=("stg", s % 2), group=True)

    def sumsq_sq(self, c):
        sq = self.SBb[c % 2]
        self.act(sq[:, :], self.xT[:, c, :], AF.Square, [("xT", c)], [("SB", c % 2)])

    def sumsq_mm(self, c):
        sq = self.SBb[c % 2]
        self.mm(self.PS[2][:, :], self.ones_b[:, :], sq[:, :], c == 0, c == 7, [("SB", c % 2), "ones_b"], [("B", 2)])

    def norm(self, gname, presummed=False):
        g = self.cg[gname]
        if not presummed:
            for c in range(8):
                self.sumsq_sq(c)
                self.sumsq_mm(c)
        sd, rs = self.SF[6], self.SF[7]
        self.rsqrt_act(rs[:, :], self.PS[2][:, :], 1.0 / D, self.cc(self.C_EPS), "c_eps", sd[:, :], [("B", 2)], ("S", 6), ("S", 7))
        for c in range(8):
            self.stt(self.hT[:, c, :], self.xT[:, c, :], g[:, c:c + 1], rs[:, :], ALU.mult, ALU.mult,
                     [("xT", c), ("S", 7), "c_" + gname], [("hT", c)])

    def ffn(self, which):
        wg, wu, wd = f"wg{which}", f"wu{which}", f"wd{which}"
        self.norm(f"g{which}", presummed=(which == 2))
        hh = self.arena
        groups = [(0, 4), (4, 4), (8, 4), (12, 4), (16, 4), (20, 2)]
        for (j0, nj) in groups:
            sg, rg = self.wslab(wg, j0 * 128, nj * 128)
            su, ru = self.wslab(wu, j0 * 128, nj * 128)
            for jj in range(nj):
                j = j0 + jj
                bg, bu = (3, 4) if j % 2 == 0 else (5, 6)
                for c in range(8):
                    self.mm(self.PS[bg][:, :], sg[:, c, jj * 128:(jj + 1) * 128], self.hT[:, c, :], c == 0, c == 7,
                            [rg, ("hT", c)], [("B", bg)])
                for c in range(8):
                    self.mm(self.PS[bu][:, :], su[:, c, jj * 128:(jj + 1) * 128], self.hT[:, c, :], c == 0, c == 7,
                            [ru, ("hT", c)], [("B", bu)])
                sl = j % 2
                self.act(self.SF[sl][:, :], self.PS[bg][:, :], AF.Silu, [("B", bg)], [("S", sl)])
                self.tt("dve", hh[:, j, :], self.SF[sl][:, :], self.PS[bu][:, :], ALU.mult, [("S", sl), ("B", bu)], [("A", j)])
        for c in range(8):
            src = self.w_bf[wd][:, c * 128:(c + 1) * 128].rearrange("(j p) m -> p j m", p=128)
            sdv, rd = self.slab(wd, c // 4, src, (NFC, 128))
            b = c % 2
            for j in range(NFC):
                self.mm(self.PS[b][:, :], sdv[:, j, :], hh[:, j, :], j == 0, j == NFC - 1, [rd, ("A", j)], [("B", b)])
            if which == 1 and c > 0:
                self.sumsq_mm(c - 1)
            self.stt(self.xT[:, c, :], self.PS[b][:, :], 0.5, self.xT[:, c, :], ALU.mult, ALU.add, [("B", b), ("xT", c)], [("xT", c)])
            if which == 1:
                self.sumsq_sq(c)
        if which == 1:
            self.sumsq_mm(7)

    def sgA(self, c):
        return self.arena[:, c, :], ("A", c)

    def sgB(self, c):
        return self.arena[:, 8 + c, :], ("A", 8 + c)

    def uT(self, c):
        return self.arena[:, 16 + c, :], ("A", 16 + c)

    def mixer_in(self, j):
        self.norm("gm", presummed=True)
        bank_rr = 0
        for grp in range(5):
            sv, rsl = self.wslab("win", grp * 512, 512)
            for ci in range(4):
                cc = grp * 4 + ci
                b = 3 + (bank_rr % 4)
                bank_rr += 1
                for c in range(8):
                    self.mm(self.PS[b][:, :], sv[:, c, ci * 128:(ci + 1) * 128], self.hT[:, c, :], c == 0, c == 7,
                            [rsl, ("hT", c)], [("B", b)])
                if cc < 8:
                    o, r = self.sgA(cc)
                    self.act(o, self.PS[b][:, :], AF.Sigmoid, [("B", b)], [r])
                elif cc < 16:
                    o, r = self.sgB(cc - 8)
                    self.act(o, self.PS[b][:, :], AF.Sigmoid, [("B", b)], [r])
                else:
                    o, r = self.uT(cc - 16)
                    self.act(o, self.PS[b][:, :], AF.Gelu, [("B", b)], [r])
        kinds = ["vg", "q0", "q1", "k0", "k1", "v0", "v1"]
        units = [(ti, kind, s) for ti, kind in enumerate(kinds) for s in range(4)]
        stageB, stageC = {}, {}
        nqk = 0
        vgB = []
        sv = rsl = None
        for t in range(len(units) + 3):
            if t - 3 in stageC:
                stageC.pop(t - 3)()
            A1 = A2 = None
            if t < len(units):
                ti, kind, s = units[t]
                if s == 0:
                    sv, rsl = self.wslab("win", 2560 + ti * 512, 512)
                b = 3 + (t % 5)
                st_ = t % 4
                for c in range(8):
                    self.mm(self.PS[b][:, :], self.hT[:, c, s * 128:(s + 1) * 128], sv[:, c, :], c == 0, c == 7,
                            [rsl, ("hT", c)], [("B", b)])
                if kind == "vg":
                    A1, Bv = self.vg_stages(b, s, s)
                    vgB.append(Bv)
                    if s == 3:
                        A2 = self.vg_rstd_all
                elif kind[0] in "qk":
                    A1, A2, B, C = self.qk_stages(b, s, int(kind[1]), kind[0], j, st_, nqk % 3)
                    nqk += 1
                    stageB[t], stageC[t] = B, C
                else:
                    half = int(kind[1])
                    self.copy("act", self.Vc[:, s, half * 512:(half + 1) * 512], self.PS[b][:, :], [("B", b)], [("Vc", s)])
            if t in (4, 5):
                vgB.pop(0)()
                vgB.pop(0)()
            if t - 2 in stageB:
                stageB.pop(t - 2)()
            if A1 is not None:
                A1()
            if A2 is not None:
                A2()
        if j < self.nt - 1:
            self.dma("sp", self.kc_d[:, :, j * TT:(j + 1) * TT].rearrange("h p t -> p h t"), self.KTc[:, :, :],
                     [("KTc", h) for h in range(8)], [("kc", j)], key="kcw")
            self.dma("sp", self.vc_d[j * TT:(j + 1) * TT, :].rearrange("(s p) d -> p s d", p=128), self.Vc[:, :, :],
                     [("Vc", s) for s in range(4)], [("vc", j)], key="vcw")

    def vg_stages(self, b, s, i):
        vgf = self.SF[2 * i + 1]
        r0 = ("S", 2 * i + 1)
        smr = ("sm", i)
        o = i * 64
        st6 = self.sm[:, o:o + 6]
        mv = self.sm[:, o + 8:o + 10]
        rs = self.sm[:, o + 11:o + 12]

        def A1():
            self.act(vgf[:, :], self.PS[b][:, :], AF.Gelu, [("B", b)], [r0])
            self.P.add("dve", lambda e: e.bn_stats(out=st6, in_=vgf[:, :]), [r0], [smr])
            self.P.add("dve", lambda e: e.bn_aggr(out=mv, in_=st6), [smr], [smr])

        def B():
            self.ts("dve", vgf[:, :], vgf[:, :], self.sm[:, o + 8:o + 9], rs, ALU.subtract, ALU.mult, [r0, smr], [r0])
            self.tt("dve", vgf[:, :], vgf[:, :], self.lng_t[:, :], ALU.mult, [r0, "c_lng_t"], [r0])
            self.tt("dve", self.vn[:, s, :], vgf[:, :], self.lnb_t[:, :], ALU.add, [r0, "c_lnb_t"], [("vn", s)])
        return A1, B

    def vg_rstd_all(self):
        smv = self.sm[:, :].rearrange("p (i c) -> p i c", c=64)
        var, sd, rs = smv[:, :, 9], smv[:, :, 10], smv[:, :, 11]
        allsm = [("sm", i) for i in range(4)]
        self.act(sd, var, AF.Ln, allsm + ["c_epsln"], allsm, bias=self.cc(self.C_EPSLN), scale=1.0)
        self.act(rs, sd, AF.Exp, allsm, allsm, scale=-0.5)

    def qk_stages(self, b, s, half, kind, j, st_, tb):
        ps = self.PS[b]
        i0 = st_ * 2
        sq, zn = self.SF[i0], self.SF[i0 + 1]
        rq, rn = ("S", i0), ("S", i0 + 1)
        smr = ("sm", st_)
        o = st_ * 64
        ki = 0 if kind == "q" else 1
        ss = self.sm[:, o + 16:o + 24]
        sd = self.sm[:, o + 24:o + 32]
        rs = self.sm[:, o + 32:o + 40]

        def A1():
            self.act(sq[:, :], ps[:, :], AF.Square, [("B", b)], [rq])
            self.P.add("dve", lambda e: e.tensor_reduce(out=ss, in_=sq[:, :].rearrange("p (g e) -> p g e", e=64), axis=AX.X, op=ALU.add),
                       [rq], [smr])

        def A2():
            self.rsqrt_act(rs, ss, 1.0 / 64, self.cc(self.C_EPS), "c_eps", sd, [smr], smr, smr)
            self.tt("dve", zn[:, :].rearrange("p (g e) -> p g e", e=64), ps[:, :].rearrange("p (g e) -> p g e", e=64),
                    rs.unsqueeze(2).to_broadcast([128, 8, 64]), ALU.mult, [("B", b), smr], [rn])

        def B():
            st = j * 4 + s
            z3 = zn[:, :].rearrange("p (g e) -> p g e", e=64)
            r1, r2, z16 = z3[:, :, 0:8], z3[:, :, 8:16], z3[:, :, 0:16]
            T = self.ropeT[kind]
            tres = "ropeT" + kind
            cg = T[:, st:st + 1, 0:16].to_broadcast([128, 8, 16])
            nsg = T[:, st:st + 1, 16:24].to_broadcast([128, 8, 8])
            psg = T[:, st:st + 1, 24:32].to_broadcast([128, 8, 8])
            t0 = self.rt[st_ * 2][:, :].rearrange("p (g e) -> p g e", e=16)
            u = self.rt[st_ * 2 + 1][:, :].rearrange("p (g e) -> p g e", e=16)
            rt0, rt1 = ("rt", st_, 0), ("rt", st_, 1)
            self.tt("dve", t0, z16, cg, ALU.mult, [rn, tres], [rt0])
            self.tt("dve", u[:, :, 0:8], r2, nsg, ALU.mult, [rn, tres], [rt1])
            self.tt("dve", u[:, :, 8:16], r1, psg, ALU.mult, [rn, tres], [rt1])
            self.tt("dve", z16, t0, u, ALU.add, [rt0, rt1], [rn])

        def C():
            for hh in range(4):
                self.tr(self.PS[tb][:, hh * 128:(hh + 1) * 128], zn[:, hh * 128:(hh + 1) * 128], [rn], [("B", tb)])
            dst = self.QT if kind == "q" else self.KTc
            dn = "QT" if kind == "q" else "KTc"
            self.act(dst[:, half * 4:(half + 1) * 4, s * 128:(s + 1) * 128], self.PS[tb][:, :].rearrange("p (h t) -> p h t", h=4),
                     AF.Identity, [("B", tb), "c_gqk_col"], [(dn, h) for h in range(half * 4, half * 4 + 4)], scale=self.gcol[:, ki:ki + 1])
        return A1, A2, B, C

    def gmlp(self):
        for s in range(4):
            b = 7 if s % 2 == 0 else 2
            for g in range(8):
                gp, half = g // 2, g % 2
                self.mm(self.PS[b][half * 64:(half + 1) * 64, gp * 128:(gp + 1) * 128], self.vn[:, s, g * 64:(g + 1) * 64],
                        self.wsT[:, g, :], True, True, [("vn", s), "wsT"], [("B", b)])
            tmp = self.SF[5]
            self.tt("dve", tmp[:, :], self.PS[b][:, :], self.bsT[:, :], ALU.add, [("B", b), "c_bsT"], [("S", 5)])
            ur = [("A", 16 + c) for c in range(4)]
            self.tt("dve", self.yaT[:, :, s * 128:(s + 1) * 128], tmp[:, :].rearrange("p (g t) -> p g t", g=4),
                    self.arena[:, 16:20, s * 128:(s + 1) * 128], ALU.mult, [("S", 5)] + ur, [("yaT", c) for c in range(4)])

    def attention(self, j):
        nprev = 4 * j
        nkt = nprev + 4
        post = None
        sched = {1: 1, 2: 2, 3: 3} if nkt < 8 else {1: 1, 4: 2, 5: 3}
        for h in range(8):
            kb = h % 2
            if j > 0:
                self.dma("sp", self.kp[kb][:, 0:nprev * 128], self.kc_d[h, :, 0:nprev * 128], [("kc", jj) for jj in range(j)],
                         [("kp", kb)], key=("kp", kb))
                self.dma("sp", self.vp[kb][:, 0:nprev, :], self.vc_d[0:nprev * 128, h * 128:(h + 1) * 128].rearrange("(k p) d -> p k d", p=128),
                         [("vc", jj) for jj in range(j)], [("vp", kb)], key=("vp", kb))

            def kv(kt):
                if kt < nprev:
                    K = self.kp[kb][:, kt * 128:(kt + 1) * 128]
                    V = self.vp[kb][:, kt, :]
                    return K, V, [("kp", kb)], [("vp", kb)]
                a = kt - nprev
                return self.KTc[:, h, a * 128:(a + 1) * 128], self.Vc[:, a, h * 128:(h + 1) * 128], [("KTc", h)], [("Vc", a)]

            def smm(kt):
                K, V, rk, rv = kv(kt)
                sbk = kt % 2
                self.mm(self.PS[2 * sbk][:, :], K[0:64, :], self.QT[0:64, h, :], True, True, rk + [("QT", h)], [("B", 2 * sbk)])
                self.mm(self.PS[2 * sbk + 1][:, :], K[64:128, :], self.QT[64:128, h, :], True, True, rk + [("QT", h)], [("B", 2 * sbk + 1)])

            smm(0)
            for kt in range(nkt):
                if kt + 1 < nkt:
                    smm(kt + 1)
                if kt == 0 and post is not None:
                    post[0]()
                K, V, rk, rv = kv(kt)
                sbk = kt % 2
                pt = self.PTA[:, sbk, :, :]
                rp = ("PT", sbk)
                self.act(pt, self.PSA[:, 2 * sbk * TT:(2 * sbk + 2) * TT].rearrange("p (m t) -> p m t", m=2), AF.Exp,
                         [("B", 2 * sbk), ("B", 2 * sbk + 1)], [rp])
                if post is not None and kt in sched:
                    post[sched[kt]]()
                    if sched[kt] == 3:
                        post = None
                if kt >= nprev:
                    a = kt - nprev
                    w = (a + 1) * 128
                    self.tt("dve", pt[:, :, 0:w], pt[:, :, 0:w], self.maskA[:, a:a + 1, 0:w].to_broadcast([128, 2, w]), ALU.mult,
                            [rp, "maskA"], [rp])
                p1, p2 = self.PTA[:, sbk, 0, :], self.PTA[:, sbk, 1, :]
                first, last = kt == 0, kt == nkt - 1
                acc = self.SF[5]
                if first:
                    self.copy("dve", acc[:, :], p1, [rp], [("S", 5)])
                else:
                    self.tt("dve", acc[:, :], acc[:, :], p1, ALU.add, [("S", 5), rp], [("S", 5)])
                self.mm(self.PS[4][:, :], V, p1, first, last, rv + [rp], [("B", 4)])
                self.mm(self.PS[5][:, :], V, p2, first, last, rv + [rp], [("B", 5)])
                self.mm(self.PS[7][:, :], self.ones_b[:, :], p2, first, last, ["ones_b", rp], [("B", 7)])
            post = self.attn_post(h)
        for kt in (0, 1, 2, 3):
            post[kt]()

    def attn_post(self, h):
        ra, rb, t1, t2 = self.SF[0], self.SF[1], self.SF[2], self.SF[3]
        o = self.SF[4]
        osq = self.SBb[0]
        def s0():
            self.mm(self.PS[6][:, :], self.ones_f[:, :], self.SF[5][:, :], True, True, ["ones_f", ("S", 5)], [("B", 6)])
        self.copy("dve", t1[:, :], self.PS[4][:, :], [("B", 4)], [("S", 2)])
        self.copy("dve", t2[:, :], self.PS[5][:, :], [("B", 5)], [("S", 3)])
        self.act(rb[:, :], self.PS[7][:, :], AF.Ln, [("B", 7)], [("S", 1)])

        def s1():
            self.act(ra[:, :], self.PS[6][:, :], AF.Ln, [("B", 6)], [("S", 0)])
            self.act(ra[:, :], ra[:, :], AF.Exp, [("S", 0)], [("S", 0)], scale=-1.0)
            self.act(rb[:, :], rb[:, :], AF.Exp, [("S", 1)], [("S", 1)], scale=-1.0)
            self.tt("dve", t1[:, :], t1[:, :], ra[:, :], ALU.mult, [("S", 2), ("S", 0)], [("S", 2)])
            self.tt("dve", t2[:, :], t2[:, :], rb[:, :], ALU.mult, [("S", 3), ("S", 1)], [("S", 3)])
            self.stt(o[:, :], t2[:, :], self.cc(self.C_NEGLAM), t1[:, :], ALU.mult, ALU.add, [("S", 2), ("S", 3), "c_neglam"], [("S", 4)])
            self.tt("dve", osq[:, :], o[:, :], o[:, :], ALU.mult, [("S", 4)], [("SB", 0)])

        def s2():
            self.mm(self.PS[0][:, :], self.ones_b[:, :], osq[:, :], True, True, ["ones_b", ("SB", 0)], [("B", 0)])
            self.act(self.SF[6][:, :], self.PS[0][:, :], AF.Ln, [("B", 0), "c_eps"], [("S", 6)], bias=self.cc(self.C_EPS), scale=1.0 / 128)

        def s3():
            self.act(self.SF[7][:, :], self.SF[6][:, :], AF.Exp, [("S", 6)], [("S", 7)], scale=-0.5)
            self.stt(self.ybT[:, h, :], o[:, :], self.cc(self.C_GSUB), self.SF[7][:, :], ALU.mult, ALU.mult,
                     [("S", 4), ("S", 7), "c_subln"], [("ybT", h)])
        return {0: s0, 1: s1, 2: s2, 3: s3}

    def mixer_out(self):
        pav, rpa = self.wslab("pa", 0, 1024)
        m = self.hT
        pbv = [None, None]
        wov = [None, None]
        for half in range(2):
            pbv[half] = self.wslab("pb", half * 512, 512)
            for ci in range(4):
                c = half * 4 + ci
                ba, bb = (0, 2) if c % 2 == 0 else (1, 3)
                for kc in range(4):
                    self.mm(self.PS[ba][:, :], pav[:, kc, c * 128:(c + 1) * 128], self.yaT[:, kc, :], kc == 0, kc == 3,
                            [rpa, ("yaT", kc)], [("B", ba)])
                sv, rsl = pbv[half]
                for kc in range(8):
                    self.mm(self.PS[bb][:, :], sv[:, kc, ci * 128:(ci + 1) * 128], self.ybT[:, kc, :], kc == 0, kc == 7,
                            [rsl, ("ybT", kc)], [("B", bb)])
                ga, rga = self.sgA(c)
                gb, rgb = self.sgB(c)
                ta, tb_ = self.SF[0 + c % 2], self.SF[2 + c % 2]
                self.tt("dve", ta[:, :], self.PS[ba][:, :], ga, ALU.mult, [("B", ba), rga], [("S", 0 + c % 2)])
                self.tt("dve", tb_[:, :], self.PS[bb][:, :], gb, ALU.mult, [("B", bb), rgb], [("S", 2 + c % 2)])
                self.tt("dve", m[:, c, :], ta[:, :], tb_[:, :], ALU.add, [("S", 0 + c % 2), ("S", 2 + c % 2)], [("hT", c)])
        for half in range(2):
            sv, rsl = self.wslab("wo", half * 512, 512)
            for ci in range(4):
                c = half * 4 + ci
                b = 4 + c % 2
                for kc in range(8):
                    self.mm(self.PS[b][:, :], sv[:, kc, ci * 128:(ci + 1) * 128], m[:, kc, :], kc == 0, kc == 7,
                            [rsl, ("hT", kc)], [("B", b)])
                if c > 0:
                    self.sumsq_mm(c - 1)
                self.tt("dve", self.xT[:, c, :], self.PS[b][:, :], self.xT[:, c, :], ALU.add, [("B", b), ("xT", c)], [("xT", c)])
                self.sumsq_sq(c)
        self.sumsq_mm(7)

    def build(self, stages=("ffn1", "mix", "ffn2")):
        self.setup()
        self.load_x_dma(0)
        for j in range(self.nt):
            self.load_x_tr(j)
            if "ffn1" in stages:
                self.ffn(1)
            if j == 0:
                self.dbg("x1", self.xT[:, :, :], [128, 8, TT], [("xT", c) for c in range(8)])
            if "mix" in stages:
                self.mixer_in(j)
                if j == 0:
                    self.dbg("QT", self.QT[:, :, :], [128, 8, TT], [("QT", h) for h in range(8)], BF16)
                    self.dbg("KT", self.KTc[:, :, :], [128, 8, TT], [("KTc", h) for h in range(8)], BF16)
                    self.dbg("V", self.Vc[:, :, :], [128, 4, D], [("Vc", s) for s in range(4)], BF16)
                    self.dbg("vn", self.vn[:, :, :], [128, 4, 512], [("vn", s) for s in range(4)], BF16)
                    self.dbg("gu", self.arena[:, 0:20, :], [128, 20, TT], [("A", c) for c in range(20)], BF16)
                self.gmlp()
                self.attention(j)
                if j == 0:
                    self.dbg("yaT", self.yaT[:, :, :], [128, 4, TT], [("yaT", c) for c in range(4)], BF16)
                    self.dbg("ybT", self.ybT[:, :, :], [128, 8, TT], [("ybT", c) for c in range(8)], BF16)
                self.mixer_out()
                if j + 1 < self.nt:
                    self.load_x_dma(j + 1)
                if j == 0:
                    self.dbg("x2", self.xT[:, :, :], [128, 8, TT], [("xT", c) for c in range(8)])
            if "ffn2" in stages:
                self.ffn(2)
            self.store_x(j)
        self.P.add("sp", None, ["OUT"], ())
        self.P.finalize()
        self.emit()
        return self.nc

    def emit(self):
        nc, P = self.nc, self.P
        with ExitStack() as es:
            sems = {}
            for i, k in enumerate(P.sem_keys()):
                sems[k] = es.enter_context(nc.semaphore(f"s{i}"))
            block = es.enter_context(nc.Block())
            block.tensor(lambda e: P.emit_engine("pe", e, sems))
            block.scalar(lambda e: P.emit_engine("act", e, sems))
            block.vector(lambda e: P.emit_engine("dve", e, sems))
            block.gpsimd(lambda e: P.emit_engine("pool", e, sems))
            block.sync(lambda e: P.emit_engine("sp", e, sems))


def prep_common(inp):
    f = lambda k: np.ascontiguousarray(np.asarray(inp[k], dtype=np.float32)[0])
    c = {
        "wg1": f("ffn1_w_gate"), "wu1": f("ffn1_w_up"), "wd1": f("ffn1_w_down"),
        "win": f("w_in"), "pa": f("a_w_proj"), "pb": f("b_w_proj"), "wo": f("w_out"),
        "wg2": f("ffn2_w_gate"), "wu2": f("ffn2_w_up"), "wd2": f("ffn2_w_down"),
    }
    col = lambda k: np.ascontiguousarray(f(k).reshape(8, 128).T)
    c["g1"], c["gm"], c["g2"] = col("ffn1_norm"), col("mix_norm"), col("ffn2_norm")
    bc = lambda v: np.ascontiguousarray(np.broadcast_to(v[None, :], (128, v.shape[0])))
    c["gq_t"] = bc(f("b_q_norm")[:16])
    c["gk_t"] = bc(f("b_k_norm")[:16])
    c["gqk_col"] = np.ascontiguousarray(np.stack([np.tile(f("b_q_norm"), 2), np.tile(f("b_k_norm"), 2)], axis=1))
    c["lng_t"] = bc(f("a_ln_gain"))
    c["lnb_t"] = bc(f("a_ln_bias"))
    bs = f("a_b_s")
    bsT = np.empty((128, 4, 128), np.float32)
    for p in range(128):
        for gp in range(4):
            bsT[p, gp, :] = bs[2 * gp + p // 64, :]
    c["bsT"] = bsT.reshape(128, 512)
    c["ws"] = f("a_w_s")
    lam = np.concatenate([f("b_lambda_q1"), f("b_lambda_k1"), f("b_lambda_q2"), f("b_lambda_k2")])
    c["lamv"] = bc(lam)
    c["subln"] = np.ascontiguousarray(f("b_subln").reshape(128, 1))
    return c


_NC_CACHE = {}


def run(inputs, debug=(), stages=("ffn1", "mix", "ffn2"), cores=None):
    x = np.asarray(inputs["x"], dtype=np.float32)
    pos = np.asarray(inputs["positions"]).astype(np.int32)
    B, S, _ = x.shape
    nt = S // TT
    key = (nt, tuple(debug), tuple(stages))
    if key not in _NC_CACHE:
        _NC_CACHE[key] = Builder(nt, debug)
        _NC_CACHE[key].build(stages)
    bld = _NC_CACHE[key]
    common = prep_common(inputs)
    in_maps = []
    for b in range(B):
        m = dict(common)
        m["x"] = np.ascontiguousarray(x[b])
        m["pos_t"] = np.ascontiguousarray(pos[b].reshape(S // 128, 128).T)
        in_maps.append(m)
    res = run_bass_kernel_spmd(bld.nc, in_maps, core_ids=list(range(B)))
    return res


def kernel(**inputs):
    res = run(inputs)
    return np.stack([np.asarray(r["out"], dtype=np.float32) for r in res.results], axis=0)
```
